# Optimizing a Trainium2 kernel written in Bass

```python
import math
import jax, jax.numpy as jnp
from jax import lax
import numpy as np

D_MODEL = 1024
BATCH = 4
SEQ = 8192
DEPTH = 1

C_CONV = 512
CONV_WIDTH = 31
N_HEADS = 4
HEAD_DIM = 64
V_DIM = 2 * HEAD_DIM
ATT_WIDTH = N_HEADS * V_DIM
Q_BLOCK = 128
D_FF = 4 * D_MODEL
PLE_DIM = 256
LN_EPS = 1e-5
DEEPNORM_ALPHA = (2.0 * DEPTH) ** 0.25
DEEPNORM_BETA = (8.0 * DEPTH) ** -0.25
N_GLU = 2 * C_CONV
N_QK = N_HEADS * 2 * HEAD_DIM
N_V = ATT_WIDTH
N_GATE = 2 * D_MODEL
N_IN = N_GLU + 2 * N_QK + N_V + N_GATE
SPLITS = (N_GLU, N_GLU + N_QK, N_GLU + 2 * N_QK, N_GLU + 2 * N_QK + N_V, N_GLU + 2 * N_QK + N_V + D_MODEL)
NEG_INF = -1e30

kernel_name = "hybrid_conformer_conv_diff_attn_deepnorm"


def layer_norm(x, g, b):
    xf = x.astype(jnp.float32)
    mu = jnp.mean(xf, axis=-1, keepdims=True)
    var = jnp.mean(jnp.square(xf - mu), axis=-1, keepdims=True)
    y = (xf - mu) * lax.rsqrt(var + LN_EPS)
    return (y * g.astype(jnp.float32) + b.astype(jnp.float32)).astype(x.dtype)


def rms_norm(x, g):
    xf = x.astype(jnp.float32)
    y = xf * lax.rsqrt(jnp.mean(jnp.square(xf), axis=-1, keepdims=True) + LN_EPS)
    return (y * g.astype(jnp.float32)).astype(x.dtype)


def alibi_slopes():
    return 2.0 ** (-8.0 * jnp.arange(1, N_HEADS + 1, dtype=jnp.float32) / N_HEADS)


def conformer_conv(z, w_dw, ln_g, ln_b, w_pw):
    a, g = jnp.split(z, 2, axis=-1)
    u = a * jax.nn.sigmoid(g)
    u = lax.conv_general_dilated(
        u, w_dw[:, None, :].astype(u.dtype), window_strides=(1,),
        padding=[(CONV_WIDTH - 1, 0)],
        dimension_numbers=("NWC", "WIO", "NWC"),
        feature_group_count=C_CONV)
    u = jax.nn.silu(layer_norm(u, ln_g, ln_b))
    return u @ w_pw


def diff_attention(q, k, v, lam):
    B, S = q.shape[0], q.shape[1]
    nb = S // Q_BLOCK
    scale = HEAD_DIM ** -0.5
    qb = (q * scale).reshape(B, nb, Q_BLOCK, N_HEADS, 2, HEAD_DIM).transpose(1, 0, 3, 4, 2, 5)
    kt = k.transpose(0, 2, 3, 1, 4)
    vt = v.transpose(0, 2, 1, 3)
    slopes = alibi_slopes()
    k_pos = jnp.arange(S)

    def block(args):
        q_blk, blk = args
        q_pos = blk * Q_BLOCK + jnp.arange(Q_BLOCK)
        dist = q_pos[:, None] - k_pos[None, :]
        bias = -slopes[:, None, None] * dist.astype(jnp.float32)
        s = jnp.einsum("bhcqd,bhcsd->bhcqs", q_blk, kt).astype(jnp.float32) + bias[None, :, None]
        s = jnp.where(dist >= 0, s, NEG_INF)
        pr = jax.nn.softmax(s, axis=-1)
        a = pr[:, :, 0] - lam * pr[:, :, 1]
        return jnp.einsum("bhqs,bhsv->bhqv", a.astype(vt.dtype), vt)

    o = lax.map(block, (qb, jnp.arange(nb)))
    return o.transpose(1, 0, 3, 2, 4).reshape(B, S, N_HEADS, V_DIM)


def setup_inputs(seed: int = 0) -> dict:
    key = jax.random.key(seed)
    ks = jax.random.split(key, 32)
    f32 = jnp.float32
    nrm = lambda k, shape, s: jax.random.normal(k, shape, f32) * s
    d_s = D_MODEL ** -0.5
    w_in = jnp.concatenate([
        nrm(ks[2], (DEPTH, D_MODEL, N_GLU + 2 * N_QK), d_s),
        nrm(ks[3], (DEPTH, D_MODEL, N_V), d_s * DEEPNORM_BETA),
        nrm(ks[4], (DEPTH, D_MODEL, N_GATE), d_s),
    ], axis=-1)
    return {
        "x": jax.random.normal(ks[0], (BATCH, SEQ, D_MODEL), f32),
        "p": jax.random.normal(ks[1], (DEPTH, BATCH, SEQ, PLE_DIM), f32),
        "ln0_g": 1.0 + nrm(ks[5], (D_MODEL,), 0.02),
        "ln0_b": nrm(ks[6], (D_MODEL,), 0.02),
        "w_in": w_in,
        "conv_w": nrm(ks[7], (DEPTH, CONV_WIDTH, C_CONV), CONV_WIDTH ** -0.5),
        "conv_ln_g": 1.0 + nrm(ks[8], (DEPTH, C_CONV), 0.02),
        "conv_ln_b": nrm(ks[9], (DEPTH, C_CONV), 0.02),
        "w_conv_out": nrm(ks[10], (DEPTH, C_CONV, D_MODEL), C_CONV ** -0.5 * DEEPNORM_BETA),
        "lambda_q1": nrm(ks[11], (DEPTH, HEAD_DIM), 0.1),
        "lambda_k1": nrm(ks[12], (DEPTH, HEAD_DIM), 0.1),
        "lambda_q2": nrm(ks[13], (DEPTH, HEAD_DIM), 0.1),
        "lambda_k2": nrm(ks[14], (DEPTH, HEAD_DIM), 0.1),
        "subln_g": 1.0 + nrm(ks[15], (DEPTH, ATT_WIDTH), 0.02),
        "w_attn_out": nrm(ks[16], (DEPTH, ATT_WIDTH, D_MODEL), ATT_WIDTH ** -0.5 * DEEPNORM_BETA),
        "w_o": nrm(ks[17], (DEPTH, D_MODEL, D_MODEL), d_s * DEEPNORM_BETA),
        "ln1_g": 1.0 + nrm(ks[18], (DEPTH, D_MODEL), 0.02),
        "ln1_b": nrm(ks[19], (DEPTH, D_MODEL), 0.02),
        "w_ff1": nrm(ks[20], (DEPTH, D_MODEL, D_FF), d_s * DEEPNORM_BETA),
        "w_ff2": nrm(ks[21], (DEPTH, D_FF, D_MODEL), D_FF ** -0.5 * DEEPNORM_BETA),
        "w_ple": nrm(ks[22], (DEPTH, PLE_DIM, D_MODEL), PLE_DIM ** -0.5 * DEEPNORM_BETA),
        "w_ple_gate": nrm(ks[23], (DEPTH, D_MODEL, D_MODEL), d_s),
        "ln2_g": 1.0 + nrm(ks[24], (DEPTH, D_MODEL), 0.02),
        "ln2_b": nrm(ks[25], (DEPTH, D_MODEL), 0.02),
    }


def reference(x, p, ln0_g, ln0_b, w_in, conv_w, conv_ln_g, conv_ln_b, w_conv_out,
              lambda_q1, lambda_k1, lambda_q2, lambda_k2, subln_g, w_attn_out, w_o,
              ln1_g, ln1_b, w_ff1, w_ff2, w_ple, w_ple_gate, ln2_g, ln2_b):
    B, S = x.shape[0], x.shape[1]
    h = layer_norm(x, ln0_g, ln0_b)
    for i in range(DEPTH):
        lambda_init = 0.8 - 0.6 * math.exp(-0.3 * i)
        z = h @ w_in[i]
        z_glu, z_q, z_k, z_v, g_conv, g_attn = jnp.split(z, SPLITS, axis=-1)
        y_conv = conformer_conv(z_glu, conv_w[i], conv_ln_g[i], conv_ln_b[i], w_conv_out[i])
        lam = (jnp.exp(jnp.sum(lambda_q1[i].astype(jnp.float32) * lambda_k1[i].astype(jnp.float32)))
               - jnp.exp(jnp.sum(lambda_q2[i].astype(jnp.float32) * lambda_k2[i].astype(jnp.float32)))
               + lambda_init)
        q = z_q.reshape(B, S, N_HEADS, 2, HEAD_DIM)
        k = z_k.reshape(B, S, N_HEADS, 2, HEAD_DIM)
        v = z_v.reshape(B, S, N_HEADS, V_DIM)
        o = diff_attention(q, k, v, lam)
        o = rms_norm(o, subln_g[i].reshape(N_HEADS, V_DIM)) * (1.0 - lambda_init)
        y_attn = o.reshape(B, S, ATT_WIDTH) @ w_attn_out[i]
        merged = jax.nn.sigmoid(g_conv) * y_conv + jax.nn.sigmoid(g_attn) * y_attn
        h = layer_norm(DEEPNORM_ALPHA * h + merged @ w_o[i], ln1_g[i], ln1_b[i])
        ff = jnp.square(jax.nn.relu(h @ w_ff1[i])) @ w_ff2[i]
        ple = jax.nn.sigmoid(h @ w_ple_gate[i]) * (p[i] @ w_ple[i])
        h = layer_norm(DEEPNORM_ALPHA * h + ff + ple, ln2_g[i], ln2_b[i])
    return h
```

```python
import numpy as np
import ml_dtypes
from contextlib import ExitStack
import concourse.bass as bass
import concourse.mybir as mybir
from concourse.bass_utils import run_bass_kernel_spmd

F32 = mybir.dt.float32
BF16 = mybir.dt.bfloat16
AF = mybir.ActivationFunctionType
ALU = mybir.AluOpType

D = 1024
S = 8192
TS = 512
NJ = 8
NG = 16
ALPHA = 2.0 ** 0.25
EPS = 1e-5
SLOPES = [2.0 ** (-2.0 * (h + 1)) for h in range(4)]
NEGM = -240000.0
NEGB = -30000.0
LAMBDA_INIT = 0.2

C_LN0G, C_LN0B, C_LN1G, C_LN1B, C_CG, C_CB, C_CW, C_HM, C_SG = 0, 8, 16, 24, 32, 36, 40, 164, 172
NCOL = 176
SAME_ENGINE_SYNC = True
DEBUG = None


class Ev:
    __slots__ = ("sem", "val")

    def __init__(self, sem, val):
        self.sem = sem
        self.val = val


class Buf:
    def __init__(self, name, psum=False):
        self.name = name
        self.w = None
        self.r = {}
        self.sem = None
        self.semcnt = 0
        self.psum = psum


class Eng:
    def __init__(self, nc, eng, name, is_pe=False):
        self.nc = nc
        self.e = eng
        self.name = name
        self.is_pe = is_pe
        self.sem = nc.alloc_semaphore(name="s_" + name)
        self.cnt = 0
        self.waited = {}
        self.q = []

    def wait(self, ev):
        if ev is None:
            return
        if ev.sem is self.sem and (self.is_pe or not SAME_ENGINE_SYNC):
            return
        k = id(ev.sem)
        if self.waited.get(k, 0) >= ev.val:
            return
        self.q.append((0, ev.sem, ev.val))
        self.waited[k] = ev.val

    def deps(self, reads, writes):
        for b in reads:
            self.wait(b.w)
            if b.psum:
                for e in list(b.r.values()):
                    self.wait(e)
        for b in writes:
            self.wait(b.w)
            for e in list(b.r.values()):
                self.wait(e)

    def mark(self, reads, writes, ev):
        k = id(ev.sem)
        for b in reads:
            o = b.r.get(k)
            if o is None or o.val < ev.val:
                b.r[k] = ev
        for b in writes:
            b.w = ev
            b.r = {}

    def op(self, reads, writes, fn, signal=True):
        self.deps(reads, writes)
        if signal:
            self.cnt += 1
            ev = Ev(self.sem, self.cnt)
            self.q.append((1, fn, self.sem, 1))
        else:
            ev = Ev(self.sem, self.cnt + 1)
            self.q.append((1, fn, None, 0))
        self.mark(reads, writes, ev)

    def dma(self, out, in_, reads, writes, sembuf, **kw):
        self.deps(reads, writes)
        if sembuf.sem is None:
            sembuf.sem = self.nc.alloc_semaphore(name="d_" + sembuf.name)
        e = self.e
        self.q.append((1, (lambda: e.dma_start(out=out, in_=in_, **kw)), sembuf.sem, 16))
        sembuf.semcnt += 16
        ev = Ev(sembuf.sem, sembuf.semcnt)
        self.mark(reads, writes, ev)

    def replay(self, eng):
        for it in self.q:
            if it[0] == 0:
                eng.wait_ge(it[1], it[2])
            else:
                inst = it[1]()
                if it[2] is not None:
                    inst.then_inc(it[2], it[3])


def build_program(debug=None):
    nc = bass.Bass("TRN2", target_bir_lowering=False)

    def dram_in(name, shape, dt=F32):
        return nc.dram_tensor(name, list(shape), dt, kind="ExternalInput").ap()

    xf = dram_in("xf", [S, D])
    xo = dram_in("xo", [NJ * TS, D])
    xhalo = dram_in("xhalo", [NJ * 32, D])
    po = dram_in("po", [NJ * TS, 256])
    w_in = dram_in("w_in", [D, 4608])
    w_co = dram_in("w_co", [512, D])
    w_ao = dram_in("w_ao", [512, D])
    w_o = dram_in("w_o", [D, D])
    w_ff1 = dram_in("w_ff1", [D, 4096])
    w_ff2 = dram_in("w_ff2", [4096, D])
    w_ple = dram_in("w_ple", [256, D])
    w_pg = dram_in("w_pg", [D, D])
    colp_d = dram_in("colp", [128, NCOL])
    rowp_d = dram_in("rowp", [6 * D])
    rowa_d = dram_in("rowa", [768])
    ident_d = dram_in("ident", [128, 128])
    mask_d = dram_in("masks", [128, 17 * TS], BF16)
    out_d = nc.dram_tensor("out", [NJ * TS, D], F32, kind="ExternalOutput").ap()
    dbg_d = None
    if debug is not None:
        dbg_d = nc.dram_tensor("dbg", list(debug[1]), F32, kind="ExternalOutput").ap()

    def scr(name, shape):
        return nc.dram_tensor(name, list(shape), BF16, kind="Internal").ap()

    s_wkv = scr("s_wkv", [2, 128, 4096])
    s_wq = scr("s_wq", [2, 128, 2048])
    s_glu = scr("s_glu", [2, 128, 4096])
    s_mix = scr("s_mix", [8, 128, 3072])
    s_wo = scr("s_wo", [2, 128, 4096])
    s_ff1 = scr("s_ff1", [8, 128, 4096])
    s_ff2 = scr("s_ff2", [8, 128, 4096])
    s_pg = scr("s_pg", [2, 128, 4096])
    s_ple = scr("s_ple", [128, 2048])

    bias_cols = {}
    nbias = [0]

    def bias_col(h, j, kt):
        key = (h, j, kt)
        if key not in bias_cols:
            bias_cols[key] = nbias[0]
            nbias[0] += 1
        return bias_cols[key]

    for hp in range(2):
        for j in range(NJ):
            for hh in range(2):
                for kt in range(8 * j + 8):
                    bias_col(2 * hp + hh, j, kt)
    NBIAS = nbias[0]
    btab_d = dram_in("btab", [128, NBIAS])

    es = ExitStack()
    with es:
        def sb(name, shape, dt, stack=es):
            return stack.enter_context(nc.sbuf_tensor("t_" + name, list(shape), dt))

        banks = [es.enter_context(nc.psum_tensor("pb%d" % i, [128, 512], F32)) for i in range(8)]
        bankB = [Buf("pb%d" % i, psum=True) for i in range(8)]
        blk = es.enter_context(nc.Block())
        pe = Eng(nc, nc.tensor, "pe", True)
        act = Eng(nc, nc.scalar, "act")
        dve = Eng(nc, nc.vector, "dve")
        pool = Eng(nc, nc.gpsimd, "pool")
        sp = Eng(nc, nc.sync, "sp")
        engines = [pe, act, dve, pool, sp]
        dma_bufs = []

        def DB(name):
            b = Buf(name)
            dma_bufs.append(b)
            return b

        def barrier():
            for e in engines:
                for f in engines:
                    if f.cnt > 0:
                        e.wait(Ev(f.sem, f.cnt))
                for b in dma_bufs:
                    if b.sem is not None and b.semcnt > 0:
                        e.wait(Ev(b.sem, b.semcnt))

        rot = {"i8": 0, "i4": 0}
        inflight = set()

        def nextbank(pool8=False):
            n = 8 if pool8 else 4
            key = "i8" if pool8 else "i4"
            for _ in range(n):
                i = rot[key] % n
                rot[key] += 1
                if i not in inflight:
                    return banks[i], bankB[i]
            raise RuntimeError("no free PSUM bank")

        def MM(out, lhsT, rhs, start, stop, reads, writes, signal=True, skip=False):
            pe.op(reads, writes, lambda: nc.tensor.matmul(out, lhsT=lhsT, rhs=rhs, start=start, stop=stop,
                                                          skip_group_check=skip), signal)

        def ACT(out, in_, func, reads, writes, bias=None, scale=None, accum=None):
            kw = {}
            if bias is not None:
                kw["bias"] = bias
            if scale is not None:
                kw["scale"] = scale
            if accum is not None:
                kw["accum_out"] = accum
            act.op(reads, writes, lambda: nc.scalar.activation(out=out, in_=in_, func=func, **kw))

        def TSC(eng, out, in0, s1, s2, op0, op1, reads, writes):
            e = nc.vector if eng is dve else nc.gpsimd
            if s2 is None:
                eng.op(reads, writes, lambda: e.tensor_scalar(out=out, in0=in0, scalar1=s1, scalar2=None, op0=op0))
            else:
                eng.op(reads, writes, lambda: e.tensor_scalar(out=out, in0=in0, scalar1=s1, scalar2=s2, op0=op0, op1=op1))

        def TT(eng, out, in0, in1, op, reads, writes):
            e = nc.vector if eng is dve else nc.gpsimd
            eng.op(reads, writes, lambda: e.tensor_tensor(out=out, in0=in0, in1=in1, op=op))

        def STT(out, in0, scalar, in1, op0, op1, reads, writes, accum=None):
            if accum is None:
                dve.op(reads, writes, lambda: nc.vector.scalar_tensor_tensor(out=out, in0=in0, scalar=scalar, in1=in1, op0=op0, op1=op1))
            else:
                dve.op(reads, writes, lambda: nc.vector.scalar_tensor_tensor(out=out, in0=in0, scalar=scalar, in1=in1, op0=op0, op1=op1, accum_out=accum))

        def CP(eng, out, in_, reads, writes):
            if eng is act:
                act.op(reads, writes, lambda: nc.scalar.copy(out=out, in_=in_))
            elif eng is dve:
                dve.op(reads, writes, lambda: nc.vector.tensor_copy(out=out, in_=in_))
            else:
                pool.op(reads, writes, lambda: nc.gpsimd.tensor_copy(out=out, in_=in_))

        identf = sb("identf", [128, 128], F32)
        identb = sb("identb", [128, 128], BF16)
        onesb = sb("onesb", [128, 128], BF16)
        colp = sb("colp", [128, NCOL], F32)
        OT = sb("OT", [128, 4, NJ * TS], BF16)
        mh = sb("mh", [128, 8], F32)
        epsc = sb("epsc", [128, 1], F32)
        lam = sb("lam", [128, 4], F32)
        stt_ = sb("stt", [128, 8, 16], F32)
        xts = [sb("xt%d" % i, [128, D], F32) for i in range(2)]
        xtB = [DB("xt%d" % i) for i in range(2)]
        xhb4 = sb("xhb4", [128, 4, D], BF16)
        xhbB = [Buf("xhb%d" % i) for i in range(4)]
        hTs = [sb("hT%d" % i, [128, 8, TS], BF16) for i in range(2)]
        hTB = [[Buf("hT%d_%d" % (i, c)) for c in range(8)] for i in range(2)]
        identB, onesB, colpB, mhB, lamB, gscB = DB("ident"), Buf("ones"), DB("colp"), Buf("mh"), Buf("lam"), Buf("gsc")
        identbB = Buf("identb")
        sttB = [Buf("stt%d" % i) for i in range(8)]
        OTB = [[Buf("OT%d_%d" % (h, j)) for j in range(NJ)] for h in range(4)]
        castB = {k: DB("c_" + k) for k in ["wkv0", "wkv1", "wq0", "wq1", "rest"]}
        cnt = {"xt": 0, "st": 0, "hT": 0, "sq": 0}
        stq = [sb("stq%d" % i, [128, 4, 16], F32) for i in range(2)]
        stqB = [Buf("stq%d" % i) for i in range(2)]

        sp.dma(identf[:], ident_d[:, :], [], [identB], identB)
        sp.dma(colp[:], colp_d[:, :], [], [colpB], colpB)
        CP(dve, identb[:], identf[:], [identB], [identbB])
        pool.op([], [onesB], lambda: nc.gpsimd.memset(onesb[:], 1.0))
        pool.op([], [mhB], lambda: nc.gpsimd.memset(mh[:], -0.5))
        pool.op([], [mhB], lambda: nc.gpsimd.memset(epsc[:], EPS))
        TSC(dve, colp[:, C_CW:C_CW + 124], colp[:, C_CW:C_CW + 124], 0.5, None, ALU.mult, None, [colpB], [colpB])
        TSC(dve, colp[:, C_SG:C_SG + 4], colp[:, C_SG:C_SG + 4], 1.0 - LAMBDA_INIT, None, ALU.mult, None, [colpB], [colpB])

        def kc_view(w, c0, c1):
            return w[:, c0:c1].rearrange("(k p) n -> p k n", p=128)

        def cast(dst, src, key):
            pool.dma(dst, src, [], [castB[key]], castB[key])

        for hp in range(2):
            v = s_wkv[hp].rearrange("p (k n) -> p k n", k=8)
            cast(v[:, :, 0:256], kc_view(w_in, 1536 + hp * 256, 1536 + hp * 256 + 256), "wkv%d" % hp)
            cast(v[:, :, 256:512], kc_view(w_in, 2048 + hp * 256, 2048 + hp * 256 + 256), "wkv%d" % hp)
            cast(s_wq[hp].rearrange("p (k n) -> p k n", k=8), kc_view(w_in, 1024 + hp * 256, 1024 + hp * 256 + 256), "wq%d" % hp)
        rest_casts = []

        def cast_rest(dst, src):
            rest_casts.append((dst, src))

        for a in range(2):
            cast_rest(s_glu[a].rearrange("p (k n) -> p k n", k=8), kc_view(w_in, a * 512, a * 512 + 512))
        for m in range(8):
            cast_rest(s_mix[m][:, 0:512].rearrange("p (k n) -> p k n", k=4), kc_view(w_co, m * 128, m * 128 + 128))
            cast_rest(s_mix[m][:, 512:1024].rearrange("p (k n) -> p k n", k=4), kc_view(w_ao, m * 128, m * 128 + 128))
            cast_rest(s_mix[m][:, 1024:2048].rearrange("p (k n) -> p k n", k=8), kc_view(w_in, 2560 + m * 128, 2560 + m * 128 + 128))
            cast_rest(s_mix[m][:, 2048:3072].rearrange("p (k n) -> p k n", k=8), kc_view(w_in, 3584 + m * 128, 3584 + m * 128 + 128))
        for a in range(2):
            cast_rest(s_wo[a].rearrange("p (k n) -> p k n", k=8), kc_view(w_o, a * 512, a * 512 + 512))
        for n in range(8):
            cast_rest(s_ff1[n].rearrange("p (k n) -> p k n", k=8), kc_view(w_ff1, n * 512, n * 512 + 512))
        for half in range(2):
            for kg in range(4):
                src = w_ff2[kg * 1024:(kg + 1) * 1024, half * 512:(half + 1) * 512].rearrange("(k p) n -> p k n", p=128)
                cast_rest(s_ff2[half * 4 + kg].rearrange("p (k n) -> p k n", k=8), src)
        for a in range(2):
            cast_rest(s_pg[a].rearrange("p (k n) -> p k n", k=8), kc_view(w_pg, a * 512, a * 512 + 512))
        cast_rest(s_ple.rearrange("p (k n) -> p k n", k=2), w_ple.rearrange("(k p) n -> p k n", p=128))

        def ln_stats_a(src, srcB, P):
            slot = cnt["st"] % 8
            cnt["st"] += 1
            st = stt_[0:P, slot, :]
            B_ = sttB[slot]
            dve.op([srcB], [B_], lambda: nc.vector.bn_stats(out=st[:, 0:6], in_=src[:, 0:512]))
            dve.op([srcB], [B_], lambda: nc.vector.bn_stats(out=st[:, 6:12], in_=src[:, 512:1024]))
            dve.op([B_], [B_], lambda: nc.vector.bn_aggr(out=st[:, 12:14], in_=st[:, 0:12]))
            TSC(dve, st[:, 14:15], st[:, 13:14], EPS, None, ALU.add, None, [B_], [B_])
            TT(pool, st[:, 14:15], st[:, 14:15], mh[0:P, 0:1], ALU.pow, [B_, mhB], [B_])
            return st, B_

        def ln_stats_b(st, B_):
            STT(st[:, 15:16], st[:, 12:13], -1.0, st[:, 14:15], ALU.mult, ALU.mult, [B_], [B_])
            return st[:, 14:15], st[:, 15:16], B_

        def ln_stats(src, srcB, P):
            st, B_ = ln_stats_a(src, srcB, P)
            return ln_stats_b(st, B_)

        def transposes_to(hT, hTb, gcol, bcol, pool8):
            for c in range(8):
                bk, bkB = nextbank(pool8)
                for s in range(4):
                    MM(bk[:, s * 128:(s + 1) * 128], xhb4[:, s, c * 128:(c + 1) * 128], identb[:], True, True,
                       [xhbB[s], identbB], [bkB], signal=(s == 3))
                if c % 2 == 0:
                    TSC(dve, hT[:, c, :], bk[:, :], colp[:, gcol + c:gcol + c + 1], colp[:, bcol + c:bcol + c + 1],
                        ALU.mult, ALU.add, [bkB, colpB], [hTb[c]])
                else:
                    ACT(hT[:, c, :], bk[:, :], AF.Identity, [bkB, colpB], [hTb[c]], bias=colp[:, bcol + c:bcol + c + 1],
                        scale=colp[:, gcol + c:gcol + c + 1])

        xpool = {"bufs": [(xts[0], xtB[0]), (xts[1], xtB[1])]}

        def ln_part(src_rows, save=None):
            if len(xpool["bufs"]) >= 4:
                qi = cnt["sq"] % 2
                cnt["sq"] += 1
                st, stB = stq[qi], stqB[qi]
                xs = []
                for s in range(4):
                    xt, xtb = xpool["bufs"][cnt["xt"] % len(xpool["bufs"])]
                    cnt["xt"] += 1
                    sp.dma(xt[:], src_rows[s * 128:(s + 1) * 128, :], [], [xtb], xtb)
                    dve.op([xtb], [stB], lambda xt=xt, s=s: nc.vector.bn_stats(out=st[:, s, 0:6], in_=xt[:, 0:512]))
                    dve.op([xtb], [stB], lambda xt=xt, s=s: nc.vector.bn_stats(out=st[:, s, 6:12], in_=xt[:, 512:1024]))
                    dve.op([stB], [stB], lambda s=s: nc.vector.bn_aggr(out=st[:, s, 12:14], in_=st[:, s, 0:12]))
                    xs.append((xt, xtb))
                TSC(dve, st[:, :, 14], st[:, :, 13], EPS, None, ALU.add, None, [stB], [stB])
                TT(pool, st[:, :, 14], st[:, :, 14], mh[:, 0:4], ALU.pow, [stB, mhB], [stB])
                STT(st[:, :, 15], st[:, :, 12], -1.0, st[:, :, 14], ALU.mult, ALU.mult, [stB], [stB])
                for s in range(4):
                    xt, xtb = xs[s]
                    ACT(xhb4[:, s, :], xt[:], AF.Identity, [xtb, stB], [xhbB[s]], bias=st[:, s, 15:16], scale=st[:, s, 14:15])
                    if save is not None:
                        sv, svB = save
                        TSC(dve, sv[:, s, 0:2], st[:, s, 14:16], 1.0, None, ALU.mult, None, [stB], [svB])
                return
            for s in range(4):
                k = cnt["xt"] % 2
                cnt["xt"] += 1
                sp.dma(xts[k][:], src_rows[s * 128:(s + 1) * 128, :], [], [xtB[k]], xtB[k])
                rstd, nb, stB = ln_stats(xts[k], xtB[k], 128)
                ACT(xhb4[:, s, :], xts[k][:], AF.Identity, [xtB[k], stB], [xhbB[s]], bias=nb, scale=rstd)
                if save is not None:
                    sv, svB = save
                    TSC(dve, sv[:, s, 0:1], rstd, 1.0, None, ALU.mult, None, [stB], [svB])
                    TSC(dve, sv[:, s, 1:2], nb, 1.0, None, ALU.mult, None, [stB], [svB])

        def tr_part(pool8):
            i = cnt["hT"] % 2
            cnt["hT"] += 1
            hT, hTb = hTs[i], hTB[i]
            transposes_to(hT, hTb, C_LN0G, C_LN0B, pool8)
            return hT, hTb

        def make_hT(src_rows, pool8, save=None):
            ln_part(src_rows, save)
            return tr_part(pool8)

        with ExitStack() as ka:
            KT = sb("KT", [128, 2, S], BF16, ka)
            Vt = sb("Vt", [128, 64, 2, 128], BF16, ka)
            QTs = [sb("QT%d" % i, [128, 2, TS], BF16, ka) for i in range(2)]
            wkv = sb("wkv", [128, 8, 512], BF16, ka)
            wq = sb("wq", [128, 8, 256], BF16, ka)
            PTs = [sb("PT%d" % i, [128, TS], BF16, ka) for i in range(6)]
            masks = sb("masks", [128, 17, TS], BF16, ka)
            btab = sb("btab", [128, NBIAS], F32, ka)
            NZT = 4
            ztmp = [sb("ztmp%d" % i, [128, TS], F32, ka) for i in range(NZT)]
            dd1 = sb("dd1", [128, TS], F32, ka)
            dd = sb("dd", [128, TS], F32, ka)
            trec = sb("trec", [128, TS], F32, ka)
            trec2 = sb("trec2", [128, TS], F32, ka)
            rrt = sb("rrt", [128, TS], F32, ka)
            dsq = sb("dsq", [128, TS], BF16, ka)
            rowa = sb("rowa", [128, 768], F32, ka)
            ljunk = sb("ljunk", [128, 64], F32, ka)
            for i in range(2, 4):
                xpool["bufs"].append((sb("xt%d" % i, [128, D], F32, ka), DB("xt%d" % i)))
            KTB = [Buf("KT%d" % g) for g in range(NG)]
            VB = [Buf("V%d" % g) for g in range(NG)]
            QTB = [Buf("QT%d" % i) for i in range(2)]
            wkvB, wqB = DB("wkv"), DB("wq")
            PTB = [Buf("PT%d" % i) for i in range(6)]
            masksB, btabB, rowaB = DB("masks"), DB("btab"), DB("rowa")
            ztB = [Buf("zt%d" % i) for i in range(NZT)]
            dd1B, ddB, trecB, rrtB, dsqB, ljB = Buf("dd1"), Buf("dd"), Buf("trec"), Buf("rrt"), Buf("dsq"), Buf("lj")
            trec2B = Buf("trec2")

            sp.dma(masks[:], mask_d.rearrange("p (z q) -> p z q", z=17), [], [masksB], masksB)
            sp.dma(btab[:], btab_d[:, :], [], [btabB], btabB)
            sp.dma(rowa[:], rowa_d.partition_broadcast(128), [], [rowaB], rowaB)
            STT(ljunk[:], rowa[:, 512:576], 1.0, rowa[:, 576:640], ALU.mult, ALU.mult, [rowaB], [ljB, lamB], accum=lam[:, 0:1])
            STT(ljunk[:], rowa[:, 640:704], 1.0, rowa[:, 704:768], ALU.mult, ALU.mult, [rowaB, ljB], [ljB, lamB], accum=lam[:, 1:2])
            ACT(lam[:, 0:2], lam[:, 0:2], AF.Exp, [lamB], [lamB])
            TT(dve, lam[:, 2:3], lam[:, 0:1], lam[:, 1:2], ALU.subtract, [lamB], [lamB])
            TSC(dve, lam[:, 2:3], lam[:, 2:3], LAMBDA_INIT, None, ALU.add, None, [lamB], [lamB])
            TSC(dve, lam[:, 3:4], lam[:, 2:3], -1.0, None, ALU.mult, None, [lamB], [lamB])

            pcnt = {"pt": 0, "zt": 0}
            for hp in range(2):
                sp.dma(wkv[:], s_wkv[hp].rearrange("p (k n) -> p k n", k=8), [castB["wkv%d" % hp]], [wkvB], wkvB)
                sp.dma(wq[:], s_wq[hp].rearrange("p (k n) -> p k n", k=8), [castB["wq%d" % hp]], [wqB], wqB)
                ln_part(xf[0:TS, :])
                for g in range(NG):
                    hT, hTb = tr_part(True)
                    if g + 1 < NG:
                        ln_part(xf[(g + 1) * TS:(g + 2) * TS, :])
                    else:
                        ln_part(xo[0:TS, :])
                    for hh in range(2):
                        bk, bkB = nextbank(True)
                        for kc in range(8):
                            MM(bk[:, :], wkv[:, kc, hh * 128:(hh + 1) * 128], hT[:, kc, :], kc == 0, kc == 7,
                               [wkvB, hTb[kc]], [bkB], signal=(kc == 7))
                        CP(act, KT[:, hh, g * TS:(g + 1) * TS], bk[:, :], [bkB], [KTB[g]])
                    for s in range(4):
                        bk, bkB = nextbank(True)
                        for kc in range(8):
                            MM(bk[:, 0:256], hT[:, kc, s * 128:(s + 1) * 128], wkv[:, kc, 256:512], kc == 0, kc == 7,
                               [wkvB, hTb[kc]], [bkB], signal=(kc == 7))
                        CP(dve, Vt[:, g * 4 + s, :, 0:128], bk[:, 0:256].rearrange("p (h d) -> p h d", h=2), [bkB], [VB[g]])
                    if hp == 0:
                        pool.wait(VB[g].w)
                        for _ in range(2):
                            if rest_casts:
                                d_, s_ = rest_casts.pop(0)
                                cast(d_, s_, "rest")
                tails = []
                def q_front(jq, with_ln):
                    if with_ln:
                        ln_part(xo[jq * TS:(jq + 1) * TS, :])
                    hT, hTb = tr_part(False)
                    QT, QTb = QTs[jq % 2], QTB[jq % 2]
                    for hh in range(2):
                        bk, bkB = nextbank()
                        for kc in range(8):
                            MM(bk[:, :], wq[:, kc, hh * 128:(hh + 1) * 128], hT[:, kc, :], kc == 0, kc == 7,
                               [wqB, hTb[kc]], [bkB], signal=(kc == 7))
                        CP(act, QT[:, hh, :], bk[:, :], [bkB], [QTb])

                q_front(0, False)
                for j in range(NJ):
                    QT, QTb = QTs[j % 2], QTB[j % 2]
                    nkt = 8 * j + 8
                    steps = [(hh, kt) for hh in range(2) for kt in range(nkt)]
                    sbanks = {}

                    def emit_qk(idx):
                        hh, kt = steps[idx]
                        pair = []
                        for c in range(2):
                            bk, bkB = nextbank()
                            MM(bk[:, :], KT[64 * c:64 * c + 64, hh, kt * 128:(kt + 1) * 128], QT[64 * c:64 * c + 64, hh, :],
                               True, True, [KTB[kt // 4], QTb], [bkB], signal=(c == 1))
                            pair.append((bk, bkB))
                            inflight.add(banks.index(bk))
                        sbanks[idx] = pair

                    emit_qk(0)
                    for idx, (hh, kt) in enumerate(steps):
                        if idx + 1 < len(steps):
                            emit_qk(idx + 1)
                        h = 2 * hp + hh
                        pair = sbanks.pop(idx)
                        col = bias_col(h, j, kt)
                        pts = []
                        for c in range(2):
                            bk, bkB = pair[c]
                            pi = pcnt["pt"] % 6
                            pcnt["pt"] += 1
                            pt, ptB = PTs[pi], PTB[pi]
                            src, srcB = bk, bkB
                            mi = None
                            if kt >= 8 * j:
                                mi = (kt - 8 * j) + (9 if h == 0 else 0)
                            elif h == 0:
                                mi = 8
                            if mi is not None:
                                zi = pcnt["zt"] % NZT
                                pcnt["zt"] += 1
                                TT(dve, ztmp[zi][:], bk[:, :], masks[:, mi, :], ALU.add, [bkB, masksB], [ztB[zi]])
                                src, srcB = ztmp[zi], ztB[zi]
                            ACT(pt[:, :], src[:, :], AF.Exp, [srcB, btabB], [ptB], bias=btab[:, col:col + 1], scale=0.125)
                            pts.append((pt, ptB))
                            inflight.discard(banks.index(bk))
                        for t_ in list(tails):
                            t_[0] -= 1
                            if t_[0] <= 0:
                                tails.remove(t_)
                                t_[1]()
                        for c in range(2):
                            pt, ptB = pts[c]
                            Ob, ObB = banks[4 + 2 * c], bankB[4 + 2 * c]
                            Lb, LbB = banks[5 + 2 * c], bankB[5 + 2 * c]
                            MM(Ob[:, :], Vt[:, kt, hh, 0:128], pt[:, :], kt == 0, kt == nkt - 1, [ptB, VB[kt // 4]], [ObB], signal=False)
                            MM(Lb[:, :], onesb[:, :], pt[:, :], kt == 0, kt == nkt - 1, [ptB, onesB], [LbB], signal=True)
                        if hh == 0 and kt == nkt // 2 and j + 1 < NJ:
                            q_front(j + 1, True)
                        if kt == nkt - 1:
                            ACT(trec[:, :], banks[5][:, :], AF.Ln, [bankB[5]], [trecB])
                            ACT(trec[:, :], trec[:, :], AF.Exp, [trecB], [trecB], scale=-1.0)
                            ACT(trec2[:, :], banks[7][:, :], AF.Ln, [bankB[7]], [trec2B])
                            ACT(trec2[:, :], trec2[:, :], AF.Exp, [trec2B], [trec2B], scale=-1.0)
                            TT(dve, dd1[:, :], banks[4][:, :], trec[:, :], ALU.mult, [bankB[4], trecB], [dd1B])
                            TT(dve, trec2[:, :], banks[6][:, :], trec2[:, :], ALU.mult, [bankB[6], trec2B], [trec2B])
                            STT(dd[:, :], trec2[:, :], lam[:, 3:4], dd1[:, :], ALU.mult, ALU.add, [trec2B, lamB, dd1B], [ddB])
                            TT(dve, dsq[:, :], dd[:, :], dd[:, :], ALU.mult, [ddB], [dsqB])
                            def tail(h=h, j=j, hp=hp):
                                mb, mbB = nextbank()
                                MM(mb[:, :], onesb[:, :], dsq[:, :], True, True, [dsqB, onesB], [mbB], signal=True)
                                ACT(rrt[:, :], mb[:, :], AF.Ln, [mbB, mhB], [rrtB], bias=epsc[:, 0:1], scale=1.0 / 128.0)
                                ACT(rrt[:, :], rrt[:, :], AF.Exp, [rrtB], [rrtB], scale=-0.5)
                                STT(OT[:, h, j * TS:(j + 1) * TS], dd[:, :], colp[:, C_SG + h:C_SG + h + 1], rrt[:, :], ALU.mult, ALU.mult,
                                    [ddB, colpB, rrtB], [OTB[h][j]])
                                if hp == 0:
                                    pool.wait(OTB[h][j].w)
                                    for _ in range(2):
                                        if rest_casts:
                                            d_, s_ = rest_casts.pop(0)
                                            cast(d_, s_, "rest")
                            tails.append([3, tail])
                while tails:
                    tails.pop(0)[1]()
            while rest_casts:
                d_, s_ = rest_casts.pop(0)
                cast(d_, s_, "rest")
            if debug is not None and debug[0] == "att":
                dbgB = DB("dbg")
                dbt = sb("dbt", [128, 4, 1024], F32, ka)
                dbtB = Buf("dbt")
                for q in range(4):
                    CP(dve, dbt[:], OT[:, :, q * 1024:(q + 1) * 1024], [b for hh_ in OTB for b in hh_], [dbtB])
                    pool.dma(dbg_d[:, q * 4096:(q + 1) * 4096].rearrange("p (h t) -> p h t", h=4), dbt[:], [dbtB], [dbgB], dbgB)
                pool.wait(dbgB.w)
            barrier()
        xpool["bufs"] = xpool["bufs"][:2]

        if debug is not None and debug[0] == "att":
            pool.dma(out_d[0:128, :], xts[0][:], [xtB[0]], [castB["rest"]], castB["rest"])
            pool.wait(castB["rest"].w)
            run_all(blk, pe, act, dve, pool, sp)
            return nc, bias_cols, NBIAS

        with ExitStack() as pm:
            NSLOT = 4
            ring = [sb("ring%d" % i, [128, 4096], BF16, pm) for i in range(NSLOT)]
            ringB = [DB("ring%d" % i) for i in range(NSLOT)]
            rowp = sb("rowp", [128, 6 * D], F32, pm)
            rowpB = DB("rowp")
            Y = sb("Y", [128, 4, D], F32, pm)
            YB = [DB("Y%d" % s) for s in range(4)]
            big = sb("big", [128, 8192], F32, pm)
            hid = big[:, :].bitcast(BF16).rearrange("p (k t) -> p k t", k=32)
            uT = big[:, 0:1084].bitcast(BF16).rearrange("p (m t) -> p m t", m=4)
            uTB = [Buf("uT%d" % m) for m in range(4)]
            cbf = big[:, 1088:2112].bitcast(BF16).rearrange("p (m t) -> p m t", m=4)
            cbfB = Buf("cbf")
            csq = big[:, 2112:3136].bitcast(BF16).rearrange("p (m t) -> p m t", m=4)
            csqB = Buf("csq")
            sT = big[:, 3136:4160].bitcast(BF16).rearrange("p (m t) -> p m t", m=4)
            sTB = [Buf("sT%d" % m) for m in range(4)]
            diag = [sb("diag0", [128, 31, 128], BF16, pm)] * 2
            diagB = [Buf("diag0")] * 2
            st0 = [sb("st0_%d" % i, [128, 4, 2], F32, pm) for i in range(2)]
            st0B = [Buf("st0_%d" % i) for i in range(2)]
            mT = sb("mT", [128, 8, TS], BF16, pm)
            mTB = [Buf("mT%d" % m) for m in range(8)]
            h1T = sb("h1T", [128, 8, TS], BF16, pm)
            h1TB = [Buf("h1T_%d" % c) for c in range(8)]
            hidB = [Buf("hid%d" % k) for k in range(32)]
            smallB = uTB + [cbfB, csqB] + sTB

            def alias_fence(dsts, srcs):
                for d_ in dsts:
                    for s_ in srcs:
                        evs = list(s_.r.values()) + ([s_.w] if s_.w is not None else [])
                        for ev in evs:
                            k = id(ev.sem)
                            o = d_.r.get(k)
                            if o is None or o.val < ev.val:
                                d_.r[k] = ev
            pTt = sb("pTt", [128, 2, TS], BF16, pm)
            pTB = Buf("pTt")
            pbf = sb("pbf", [128, 4, 256], BF16, pm)
            pbfB = [Buf("pbf%d" % s) for s in range(4)]
            ftmp = [sb("ftmp%d" % i, [128, TS], F32, pm) for i in range(2)] + [big[:, 5696:6208], big[:, 6208:6720]]
            ftmpB = [Buf("ftmp%d" % i) for i in range(4)]
            rtmp = [ftmp[i][:, 0:256].bitcast(BF16) for i in range(2)]
            rtmpB = [ftmpB[i] for i in range(2)]
            hTh = sb("hTh", [128, 8, 32], BF16, pm)
            hThB = Buf("hTh")
            uh = sb("uh", [128, 64], F32, pm)
            uhB = Buf("uh")
            mstat = big[:, 4160:5696].rearrange("p (a t) -> p a t", a=3)
            mstatB = Buf("mstat")
            smallB = smallB + [mstatB, ftmpB[2], ftmpB[3]]
            fcnt = {"f": 0, "r": 0, "p": 0, "n": 4}

            def ft():
                i = fcnt["f"] % fcnt["n"]
                fcnt["f"] += 1
                return ftmp[i], ftmpB[i]

            plew = sb("plew", [128, 2, 1024], BF16, pm)
            plewB = DB("plew")
            sp.dma(plew[:], s_ple.rearrange("p (k n) -> p k n", k=2), [castB["rest"]], [plewB], plewB)
            sp.dma(rowp[:], rowp_d.partition_broadcast(128), [], [rowpB], rowpB)
            TSC(dve, rowp[:, 0:4 * D], rowp[:, 0:4 * D], ALPHA, None, ALU.mult, None, [rowpB], [rowpB])

            pieces = []
            for j in range(NJ):
                pieces.append((s_glu[0], 4096))
                pieces.append((s_glu[1], 4096))
                for m in range(8):
                    pieces.append((s_mix[m], 3072))
                pieces.append((s_wo[0], 4096))
                pieces.append((s_wo[1], 4096))
                if debug is not None and debug[0] == "pre1":
                    continue
                for n in range(8):
                    pieces.append((s_ff1[n], 4096))
                for q in range(8):
                    pieces.append((s_ff2[q], 4096))
                pieces.append((s_pg[0], 4096))
                pieces.append((s_pg[1], 4096))
            pstate = {"issued": 0, "next": 0, "rel": 0}

            def issue_to(n):
                while pstate["issued"] < min(n, len(pieces)):
                    i = pstate["issued"]
                    ap_, ncol = pieces[i]
                    sl = i % NSLOT
                    sp.dma(ring[sl][:, 0:ncol], ap_, [castB["rest"]], [ringB[sl]], ringB[sl])
                    pstate["issued"] += 1

            def next_piece():
                i = pstate["next"]
                pstate["next"] += 1
                assert i < pstate["rel"] + NSLOT
                issue_to(i + 1)
                sl = i % NSLOT
                return ring[sl], ringB[sl]

            def release(n=1):
                pstate["rel"] += n
                issue_to(pstate["rel"] + NSLOT)

            issue_to(NSLOT)

            xhbh = pbf[0:32, :, :].rearrange("p s d -> p (s d)")

            def halo_ln(j):
                k = cnt["xt"] % 2
                cnt["xt"] += 1
                xh32, xh32B = xts[k][0:32, :], xtB[k]
                sp.dma(xh32, xhalo[j * 32:(j + 1) * 32, :], [], [xh32B], xh32B)
                rstd, nb, stB = ln_stats(xh32, xh32B, 32)
                ACT(xhbh, xh32, AF.Identity, [xh32B, stB], pbfB, bias=nb, scale=rstd)

            def halo_tr(pool8):
                bk, bkB = nextbank(pool8)
                for c in range(8):
                    MM(bk[:, c * 32:(c + 1) * 32], xhbh[:, c * 128:(c + 1) * 128], identb[0:32, 0:32], True, True,
                       pbfB + [identbB], [bkB], signal=(c == 7))
                for c in range(8):
                    TSC(dve, hTh[:, c, :], bk[:, c * 32:(c + 1) * 32], colp[:, C_LN0G + c:C_LN0G + c + 1],
                        colp[:, C_LN0B + c:C_LN0B + c + 1], ALU.mult, ALU.add, [bkB, colpB], [hThB])

            DH = [(0, 16), (16, 31)]
            diagHB = [Buf("diagA"), Buf("diagB")]

            def build_diag(m, hf):
                w0, w1 = DH[hf]
                n = w1 - w0
                TT(dve, diag[0][:, w0:w1, :], identb[:, :].unsqueeze(1).broadcast_to([128, n, 128]),
                   colp[:, C_CW + m * 31 + w0:C_CW + m * 31 + w1].unsqueeze(2).broadcast_to([128, n, 128]), ALU.mult,
                   [identbB, colpB], [diagHB[hf]])

            pending = []
            nxt = make_hT(xo[0:TS, :], True, (st0[0], st0B[0]))
            halo_ln(0)
            halo_tr(True)
            for j in range(NJ):
                hT, hTb = nxt
                build_diag(0, 0)
                build_diag(0, 1)
                alias_fence(smallB, hidB)
                fcnt["n"] = 4
                wa, waB = next_piece()
                wg, wgB = next_piece()
                wa3 = wa[:, :].rearrange("p (k n) -> p k n", k=8)
                wg3 = wg[:, :].rearrange("p (k n) -> p k n", k=8)
                for m in range(4):
                    ba, baB = nextbank(True)
                    bg, bgB = nextbank(True)
                    bh, bhB = nextbank(True)
                    for kc in range(8):
                        MM(ba[:, :], wa3[:, kc, m * 128:(m + 1) * 128], hT[:, kc, :], kc == 0, kc == 7, [waB, hTb[kc]], [baB], signal=(kc == 7))
                    for kc in range(8):
                        MM(bg[:, :], wg3[:, kc, m * 128:(m + 1) * 128], hT[:, kc, :], kc == 0, kc == 7, [wgB, hTb[kc]], [bgB], signal=(kc == 7))
                    for kc in range(8):
                        MM(bh[:, 0:32], wa3[:, kc, m * 128:(m + 1) * 128], hTh[:, kc, :], kc == 0, False, [waB, hThB], [bhB], signal=False, skip=True)
                    for kc in range(8):
                        MM(bh[:, 32:64], wg3[:, kc, m * 128:(m + 1) * 128], hTh[:, kc, :], False, kc == 7, [wgB, hThB], [bhB], signal=(kc == 7), skip=True)
                    f, fB = ft()
                    ACT(f[:, :], bg[:, :], AF.Tanh, [bgB], [fB], scale=0.5)
                    STT(uT[:, m, 30:30 + TS], f[:, :], 1.0, ba[:, :], ALU.add, ALU.mult, [fB, baB], [uTB[m]])
                    ACT(uh[:, 32:64], bh[:, 32:64], AF.Tanh, [bhB], [uhB], scale=0.5)
                    STT(uh[:, 0:32], uh[:, 32:64], 1.0, bh[:, 0:32], ALU.add, ALU.mult, [uhB, bhB], [uhB])
                    TSC(dve, uT[:, m, 0:30], uh[:, 2:32], colp[:, C_HM + j:C_HM + j + 1], None, ALU.mult, None, [uhB, colpB], [uTB[m]])
                    if m % 2 == 1 and pending:
                        pending.pop(0)()
                release(2)
                cbanks = []
                for m in range(4):
                    dg = diag[0]
                    cb_, cbB_ = nextbank(True)
                    for hf in range(2):
                        w0, w1 = DH[hf]
                        for w in range(w0, w1):
                            MM(cb_[:, :], dg[:, w, :], uT[:, m, w:w + TS], w == 0, w == 30, [diagHB[hf], uTB[m]], [cbB_],
                               signal=(w == w1 - 1))
                        if m + 1 < 4:
                            build_diag(m + 1, hf)
                    cbanks.append((cb_, cbB_))
                    if m % 2 == 0 and pending:
                        pending.pop(0)()
                    if m >= 1:
                        CP(act, cbf[:, m - 1, :], cbanks[m - 1][0][:, :], [cbanks[m - 1][1]], [cbfB])
                        ACT(csq[:, m - 1, :], cbanks[m - 1][0][:, :], AF.Square, [cbanks[m - 1][1]], [csqB])
                while pending:
                    pending.pop(0)()
                CP(act, cbf[:, 3, :], cbanks[3][0][:, :], [cbanks[3][1]], [cbfB])
                ACT(csq[:, 3, :], cbanks[3][0][:, :], AF.Square, [cbanks[3][1]], [csqB])
                bm, bmB = nextbank(True)
                bq, bqB = nextbank(True)
                for m in range(4):
                    MM(bm[:, :], onesb[:], cbf[:, m, :], m == 0, m == 3, [onesB, cbfB], [bmB], signal=(m == 3))
                for m in range(4):
                    MM(bq[:, :], onesb[:], csq[:, m, :], m == 0, m == 3, [onesB, csqB], [bqB], signal=(m == 3))
                ACT(mstat[:, 0, :], bm[:, :], AF.Copy, [bmB], [mstatB], scale=1.0 / 512.0)
                TT(dve, mstat[:, 1, :], mstat[:, 0, :], mstat[:, 0, :], ALU.mult, [mstatB], [mstatB])
                STT(mstat[:, 1, :], bq[:, :], 1.0 / 512.0, mstat[:, 1, :], ALU.mult, ALU.subtract, [bqB, mstatB], [mstatB])
                ACT(mstat[:, 1, :], mstat[:, 1, :], AF.Ln, [mstatB, mhB], [mstatB], bias=epsc[:, 0:1])
                ACT(mstat[:, 1, :], mstat[:, 1, :], AF.Exp, [mstatB], [mstatB], scale=-0.5)
                TT(dve, mstat[:, 2, :], mstat[:, 0, :], mstat[:, 1, :], ALU.mult, [mstatB], [mstatB])
                for m in range(4):
                    f, fB = ft()
                    TT(dve, f[:, :], cbanks[m][0][:, :], mstat[:, 1, :], ALU.mult, [cbanks[m][1], mstatB], [fB])
                    TT(dve, f[:, :], f[:, :], mstat[:, 2, :], ALU.subtract, [fB, mstatB], [fB])
                    TSC(dve, f[:, :], f[:, :], colp[:, C_CG + m:C_CG + m + 1], colp[:, C_CB + m:C_CB + m + 1], ALU.mult, ALU.add,
                        [fB, colpB], [fB])
                    f2, f2B = ft()
                    ACT(f2[:, :], f[:, :], AF.Tanh, [fB], [f2B], scale=0.5)
                    STT(sT[:, m, :], f2[:, :], 1.0, f[:, :], ALU.add, ALU.mult, [f2B, fB], [sTB[m]])
                for m in range(8):
                    wm, wmB = next_piece()
                    byc, bycB = nextbank(True)
                    bya, byaB = nextbank(True)
                    bgc, bgcB = nextbank(True)
                    bga, bgaB = nextbank(True)
                    for kc in range(8):
                        MM(bgc[:, :], wm[:, 1024 + kc * 128:1024 + (kc + 1) * 128], hT[:, kc, :], kc == 0, kc == 7, [wmB, hTb[kc]], [bgcB], signal=(kc == 7))
                    for kc in range(8):
                        MM(bga[:, :], wm[:, 2048 + kc * 128:2048 + (kc + 1) * 128], hT[:, kc, :], kc == 0, kc == 7, [wmB, hTb[kc]], [bgaB], signal=(kc == 7))
                    for kc in range(4):
                        MM(bya[:, :], wm[:, 512 + kc * 128:512 + (kc + 1) * 128], OT[:, kc, j * TS:(j + 1) * TS], kc == 0, kc == 3,
                           [wmB, OTB[kc][j]], [byaB], signal=(kc == 3))
                    for kc in range(4):
                        MM(byc[:, :], wm[:, kc * 128:(kc + 1) * 128], sT[:, kc, :], kc == 0, kc == 3, [wmB, sTB[kc]], [bycB], signal=(kc == 3))
                    f1, f1B = ft()
                    f2, f2B = ft()
                    ACT(f1[:, :], bgc[:, :], AF.Tanh, [bgcB], [f1B], scale=0.5)
                    ACT(f2[:, :], bga[:, :], AF.Tanh, [bgaB], [f2B], scale=0.5)
                    STT(f1[:, :], f1[:, :], 1.0, byc[:, :], ALU.add, ALU.mult, [f1B, bycB], [f1B])
                    STT(f2[:, :], f2[:, :], 1.0, bya[:, :], ALU.add, ALU.mult, [f2B, byaB], [f2B])
                    STT(mT[:, m, :], f1[:, :], 0.5, f2[:, :], ALU.mult, ALU.add, [f1B, f2B], [mTB[m]])
                    release()
                sv, svB = st0[j % 2], st0B[j % 2]
                for s in range(4):
                    k = cnt["xt"] % 2
                    cnt["xt"] += 1
                    sp.dma(xts[k][:], xo[j * TS + s * 128:j * TS + (s + 1) * 128, :], [], [xtB[k]], xtB[k])
                    ACT(Y[:, s, :], xts[k][:], AF.Identity, [xtB[k], svB], [YB[s]], bias=sv[:, s, 1:2], scale=sv[:, s, 0:1])
                    TT(dve, Y[:, s, :], Y[:, s, :], rowp[:, 0:D], ALU.mult, [YB[s], rowpB], [YB[s]])
                    TT(dve, Y[:, s, :], Y[:, s, :], rowp[:, D:2 * D], ALU.add, [YB[s], rowpB], [YB[s]])
                wos = [next_piece(), next_piece()]
                ln1 = []
                prev_ln = None

                def ln1_finish(p_):
                    s_, (st_, B_) = p_
                    rstd, nb, stB = ln_stats_b(st_, B_)
                    ACT(xhb4[:, s_, :], Y[:, s_, :], AF.Identity, [YB[s_], stB], [xhbB[s_]], bias=nb, scale=rstd)
                    ln1.append((rstd, nb, stB))
                for s in range(4):
                    for half in range(2):
                        wo_, woB = wos[half]
                        wo3 = wo_[:, :].rearrange("p (k n) -> p k n", k=8)
                        bk, bkB = nextbank(True)
                        for kc in range(8):
                            MM(bk[:, :], mT[:, kc, s * 128:(s + 1) * 128], wo3[:, kc, :], kc == 0, kc == 7, [woB, mTB[kc]], [bkB], signal=(kc == 7))
                        STT(Y[:, s, half * 512:(half + 1) * 512], bk[:, :], 0.5, Y[:, s, half * 512:(half + 1) * 512], ALU.mult, ALU.add,
                            [bkB, YB[s]], [YB[s]])
                    cur = (s, ln_stats_a(Y[:, s, :], YB[s], 128))
                    if prev_ln is not None:
                        ln1_finish(prev_ln)
                    prev_ln = cur
                ln1_finish(prev_ln)
                release(2)
                transposes_to(h1T, h1TB, C_LN1G, C_LN1B, True)
                for s in range(4):
                    rstd, nb, stB = ln1[s]
                    ACT(Y[:, s, :], Y[:, s, :], AF.Identity, [YB[s], stB], [YB[s]], bias=nb, scale=rstd)
                    TT(dve, Y[:, s, :], Y[:, s, :], rowp[:, 2 * D:3 * D], ALU.mult, [YB[s], rowpB], [YB[s]])
                    TT(dve, Y[:, s, :], Y[:, s, :], rowp[:, 3 * D:4 * D], ALU.add, [YB[s], rowpB], [YB[s]])
                alias_fence(hidB, smallB)
                fcnt["n"] = 2
                for n in range(8):
                    w1, w1B = next_piece()
                    w13 = w1[:, :].rearrange("p (k n) -> p k n", k=8)
                    for i in range(4):
                        bk, bkB = nextbank()
                        for kc in range(8):
                            MM(bk[:, :], w13[:, kc, i * 128:(i + 1) * 128], h1T[:, kc, :], kc == 0, kc == 7, [w1B, h1TB[kc]], [bkB], signal=(kc == 7))
                        ri = fcnt["r"] % 2
                        fcnt["r"] += 1
                        ACT(rtmp[ri][:, :], bk[:, :], AF.Relu, [bkB], [rtmpB[ri]])
                        k_ = n * 4 + i
                        TT(dve, hid[:, k_, :], rtmp[ri][:, :], rtmp[ri][:, :], ALU.mult, [rtmpB[ri]], [hidB[k_]])
                    release()
                if j + 1 < NJ:
                    ln_part(xo[(j + 1) * TS:(j + 2) * TS, :], (st0[(j + 1) % 2], st0B[(j + 1) % 2]))
                    halo_ln(j + 1)
                for half in range(2):
                    for kg in range(4):
                        w2, w2B = next_piece()
                        w23 = w2[:, :].rearrange("p (k n) -> p k n", k=8)
                        for s in range(4):
                            for k in range(8):
                                MM(banks[4 + s][:, :], hid[:, kg * 8 + k, s * 128:(s + 1) * 128], w23[:, k, :],
                                   (kg == 0 and k == 0), (kg == 3 and k == 7), [w2B, hidB[kg * 8 + k]], [bankB[4 + s]], signal=(k == 7))
                        release()
                    for s in range(4):
                        TT(dve, Y[:, s, half * 512:(half + 1) * 512], banks[4 + s][:, :], Y[:, s, half * 512:(half + 1) * 512], ALU.add,
                           [bankB[4 + s], YB[s]], [YB[s]])
                if j + 1 < NJ:
                    nxt = tr_part(False)
                    halo_tr(False)
                for s in range(4):
                    pi = cnt["xt"] % 2
                    cnt["xt"] += 1
                    sp.dma(xts[pi][:, 0:256], po[j * TS + s * 128:j * TS + (s + 1) * 128, :], [], [xtB[pi]], xtB[pi])
                    CP(act, pbf[:, s, :], xts[pi][:, 0:256], [xtB[pi]], [pbfB[s]])
                for c2 in range(2):
                    bk, bkB = nextbank()
                    for s in range(4):
                        MM(bk[:, s * 128:(s + 1) * 128], pbf[:, s, c2 * 128:(c2 + 1) * 128], identb[:], True, True, [pbfB[s], identbB], [bkB], signal=(s == 3))
                    CP(dve, pTt[:, c2, :], bk[:, :], [bkB], [pTB])
                wpl3, wplB = plew, plewB
                for half in range(2):
                    wpg, wpgB = next_piece()
                    wpg3 = wpg[:, :].rearrange("p (k n) -> p k n", k=8)
                    for s in range(4):
                        bgt, bgtB = nextbank()
                        bpl, bplB = nextbank()
                        for kc in range(8):
                            MM(bgt[:, :], h1T[:, kc, s * 128:(s + 1) * 128], wpg3[:, kc, :], kc == 0, kc == 7, [wpgB, h1TB[kc]], [bgtB], signal=(kc == 7))
                        for c2 in range(2):
                            MM(bpl[:, :], pTt[:, c2, s * 128:(s + 1) * 128], wpl3[:, c2, half * 512:(half + 1) * 512], c2 == 0, c2 == 1,
                               [wplB, pTB], [bplB], signal=(c2 == 1))
                        f, fB = ft()
                        ACT(f[:, :], bgt[:, :], AF.Tanh, [bgtB], [fB], scale=0.5)
                        STT(f[:, :], f[:, :], 1.0, bpl[:, :], ALU.add, ALU.mult, [fB, bplB], [fB])
                        STT(Y[:, s, half * 512:(half + 1) * 512], f[:, :], 0.5, Y[:, s, half * 512:(half + 1) * 512], ALU.mult, ALU.add,
                            [fB, YB[s]], [YB[s]])
                release(2)
                def ln2_store(s, j=j):
                    rstd, nb, stB = ln_stats(Y[:, s, :], YB[s], 128)
                    ACT(Y[:, s, :], Y[:, s, :], AF.Identity, [YB[s], stB], [YB[s]], bias=nb, scale=rstd)
                    TT(dve, Y[:, s, :], Y[:, s, :], rowp[:, 4 * D:5 * D], ALU.mult, [YB[s], rowpB], [YB[s]])
                    TT(dve, Y[:, s, :], Y[:, s, :], rowp[:, 5 * D:6 * D], ALU.add, [YB[s], rowpB], [YB[s]])
                    pool.dma(out_d[j * TS + s * 128:j * TS + (s + 1) * 128, :], Y[:, s, :], [YB[s]], [], YB[s])
                for s in range(4):
                    pending.append(lambda s=s, f=ln2_store: f(s))
            while pending:
                pending.pop(0)()
            for s in range(4):
                pool.wait(Ev(YB[s].sem, YB[s].semcnt))
            barrier()
        run_all(blk, pe, act, dve, pool, sp)
    return nc, bias_cols, NBIAS


def run_all(blk, pe, act, dve, pool, sp):
    @blk.sync
    def _(e):
        sp.replay(e)

    @blk.tensor
    def _(e):
        pe.replay(e)

    @blk.scalar
    def _(e):
        act.replay(e)

    @blk.vector
    def _(e):
        dve.replay(e)

    @blk.gpsimd
    def _(e):
        pool.replay(e)


def _core_tables(r, bias_cols, NBIAS):
    p = np.arange(128, dtype=np.float64)
    btab = np.zeros((128, NBIAS), np.float32)
    for (h, j, kt), col in bias_cols.items():
        g = 2 * j + r
        kpos = kt * 128 + p
        if r == 0 and kt >= 8 * j + 4:
            btab[:, col] = NEGB
        else:
            btab[:, col] = SLOPES[h] * (kpos - (g * TS + 256))
    ki = np.arange(128)[:, None]
    qi = np.arange(TS)[None, :]
    masks = np.zeros((128, 17, TS), np.float32)
    shift = np.broadcast_to(-8.0 * SLOPES[0] * (qi - 256), (128, TS))
    masks[:, 8, :] = shift
    for u in range(4):
        causal = np.where(128 * u + ki <= qi, 0.0, NEGM)
        z = u if r == 0 else 4 + u
        masks[:, z, :] = causal
    for z in range(8):
        masks[:, 9 + z, :] = masks[:, z, :] + shift
    return btab, masks.reshape(128, 17 * TS).astype(ml_dtypes.bfloat16)


def _prep_inputs(inp, bias_cols, NBIAS):
    f = lambda a: np.ascontiguousarray(np.asarray(a, dtype=np.float32))
    x = f(inp["x"])
    p = f(inp["p"])[0]
    colT = lambda v: f(v).reshape(-1, 128).T
    conv_w = f(inp["conv_w"])[0]
    cw = np.concatenate([conv_w[:, m * 128:(m + 1) * 128].T for m in range(4)], axis=1)
    rowp = np.concatenate([f(inp["ln0_g"]), f(inp["ln0_b"]), f(inp["ln1_g"])[0], f(inp["ln1_b"])[0],
                           f(inp["ln2_g"])[0], f(inp["ln2_b"])[0]])
    rowa = np.concatenate([f(inp["subln_g"])[0], f(inp["lambda_q1"])[0], f(inp["lambda_k1"])[0],
                           f(inp["lambda_q2"])[0], f(inp["lambda_k2"])[0]])
    shared = {
        "w_in": f(inp["w_in"])[0], "w_co": f(inp["w_conv_out"])[0], "w_ao": f(inp["w_attn_out"])[0],
        "w_o": f(inp["w_o"])[0], "w_ff1": f(inp["w_ff1"])[0], "w_ff2": f(inp["w_ff2"])[0],
        "w_ple": f(inp["w_ple"])[0], "w_pg": f(inp["w_ple_gate"])[0],
        "rowp": f(rowp), "rowa": f(rowa), "ident": np.eye(128, dtype=np.float32),
    }
    in_maps = []
    for c in range(8):
        b, r = c // 2, c % 2
        xb = x[b]
        tiles = xb.reshape(NG, TS, D)
        own = [2 * j + r for j in range(NJ)]
        xo = np.ascontiguousarray(tiles[own].reshape(NJ * TS, D))
        halo = np.zeros((NJ, 32, D), np.float32)
        hm = np.zeros(NJ, np.float32)
        for j, g in enumerate(own):
            if g > 0:
                halo[j] = xb[g * TS - 32:g * TS]
                hm[j] = 1.0
        colp = np.concatenate([colT(inp["ln0_g"]), colT(inp["ln0_b"]), colT(f(inp["ln1_g"])[0]), colT(f(inp["ln1_b"])[0]),
                               colT(f(inp["conv_ln_g"])[0]), colT(f(inp["conv_ln_b"])[0]), cw,
                               np.tile(hm[None, :], (128, 1)), colT(f(inp["subln_g"])[0])], axis=1)
        assert colp.shape == (128, NCOL)
        btab, masks = _core_tables(r, bias_cols, NBIAS)
        m = dict(shared)
        m.update({"xf": np.ascontiguousarray(xb), "xo": xo, "xhalo": np.ascontiguousarray(halo.reshape(NJ * 32, D)),
                  "po": np.ascontiguousarray(p[b].reshape(NG, TS, 256)[own].reshape(NJ * TS, 256)),
                  "colp": f(colp), "btab": btab, "masks": masks})
        in_maps.append(m)
    return in_maps


_CACHE = {}


def kernel(**inputs):
    if "prog" not in _CACHE:
        _CACHE["prog"] = build_program(DEBUG)
    nc, bias_cols, NBIAS = _CACHE["prog"]
    in_maps = _prep_inputs(inputs, bias_cols, NBIAS)
    res = run_bass_kernel_spmd(nc, in_maps, core_ids=list(range(8)))
    out = np.zeros((4, S, D), np.float32)
    for c in range(8):
        b, r = c // 2, c % 2
        o = np.asarray(res.results[c]["out"]).reshape(NJ, TS, D)
        for j in range(NJ):
            g = 2 * j + r
            out[b, g * TS:(g + 1) * TS] = o[j]
    if DEBUG is not None:
        _CACHE["dbg"] = [np.asarray(r_.get("dbg")) if "dbg" in r_ else None for r_ in res.results]
    return out
```

```python
import numpy as np
import ml_dtypes
from contextlib import ExitStack
import concourse.bass as bass
import concourse.mybir as mybir
from concourse.bass_utils import run_bass_kernel_spmd

F32 = mybir.dt.float32
BF16 = mybir.dt.bfloat16
AF = mybir.ActivationFunctionType
ALU = mybir.AluOpType

D = 1024
S = 8192
TS = 512
NJ = 8
NG = 16
ALPHA = 2.0 ** 0.25
EPS = 1e-5
SLOPES = [2.0 ** (-2.0 * (h + 1)) for h in range(4)]
NEGM = -240000.0
NEGB = -30000.0
LAMBDA_INIT = 0.2

C_LN0G, C_LN0B, C_LN1G, C_LN1B, C_CG, C_CB, C_CW, C_HM, C_SG = 0, 8, 16, 24, 32, 36, 40, 164, 172
NCOL = 176
SAME_ENGINE_SYNC = True
DEBUG = None


class Ev:
    __slots__ = ("sem", "val")

    def __init__(self, sem, val):
        self.sem = sem
        self.val = val


class Buf:
    def __init__(self, name, psum=False):
        self.name = name
        self.w = None
        self.r = {}
        self.sem = None
        self.semcnt = 0
        self.psum = psum


class Eng:
    def __init__(self, nc, eng, name, is_pe=False):
        self.nc = nc
        self.e = eng
        self.name = name
        self.is_pe = is_pe
        self.sem = nc.alloc_semaphore(name="s_" + name)
        self.cnt = 0
        self.waited = {}
        self.q = []

    def wait(self, ev):
        if ev is None:
            return
        if ev.sem is self.sem and (self.is_pe or not SAME_ENGINE_SYNC):
            return
        k = id(ev.sem)
        if self.waited.get(k, 0) >= ev.val:
            return
        self.q.append((0, ev.sem, ev.val))
        self.waited[k] = ev.val

    def deps(self, reads, writes):
        for b in reads:
            self.wait(b.w)
            if b.psum:
                for e in list(b.r.values()):
                    self.wait(e)
        for b in writes:
            self.wait(b.w)
            for e in list(b.r.values()):
                self.wait(e)

    def mark(self, reads, writes, ev):
        k = id(ev.sem)
        for b in reads:
            o = b.r.get(k)
            if o is None or o.val < ev.val:
                b.r[k] = ev
        for b in writes:
            b.w = ev
            b.r = {}

    def op(self, reads, writes, fn, signal=True):
        self.deps(reads, writes)
        if signal:
            self.cnt += 1
            ev = Ev(self.sem, self.cnt)
            self.q.append((1, fn, self.sem, 1))
        else:
            ev = Ev(self.sem, self.cnt + 1)
            self.q.append((1, fn, None, 0))
        self.mark(reads, writes, ev)

    def dma(self, out, in_, reads, writes, sembuf, **kw):
        self.deps(reads, writes)
        if sembuf.sem is None:
            sembuf.sem = self.nc.alloc_semaphore(name="d_" + sembuf.name)
        e = self.e
        self.q.append((1, (lambda: e.dma_start(out=out, in_=in_, **kw)), sembuf.sem, 16))
        sembuf.semcnt += 16
        ev = Ev(sembuf.sem, sembuf.semcnt)
        self.mark(reads, writes, ev)

    def replay(self, eng):
        for it in self.q:
            if it[0] == 0:
                eng.wait_ge(it[1], it[2])
            else:
                inst = it[1]()
                if it[2] is not None:
                    inst.then_inc(it[2], it[3])


def build_program(debug=None):
    nc = bass.Bass("TRN2", target_bir_lowering=False)

    def dram_in(name, shape, dt=F32):
        return nc.dram_tensor(name, list(shape), dt, kind="ExternalInput").ap()

    xf = dram_in("xf", [S, D])
    xo = dram_in("xo", [NJ * TS, D])
    xhalo = dram_in("xhalo", [NJ * 32, D])
    po = dram_in("po", [NJ * TS, 256])
    w_in = dram_in("w_in", [D, 4608])
    w_co = dram_in("w_co", [512, D])
    w_ao = dram_in("w_ao", [512, D])
    w_o = dram_in("w_o", [D, D])
    w_ff1 = dram_in("w_ff1", [D, 4096])
    w_ff2 = dram_in("w_ff2", [4096, D])
    w_ple = dram_in("w_ple", [256, D])
    w_pg = dram_in("w_pg", [D, D])
    colp_d = dram_in("colp", [128, NCOL])
    rowp_d = dram_in("rowp", [6 * D])
    rowa_d = dram_in("rowa", [768])
    ident_d = dram_in("ident", [128, 128])
    mask_d = dram_in("masks", [128, 17 * TS], BF16)
    out_d = nc.dram_tensor("out", [NJ * TS, D], F32, kind="ExternalOutput").ap()
    dbg_d = None
    if debug is not None:
        dbg_d = nc.dram_tensor("dbg", list(debug[1]), F32, kind="ExternalOutput").ap()

    def scr(name, shape):
        return nc.dram_tensor(name, list(shape), BF16, kind="Internal").ap()

    s_wkv = scr("s_wkv", [2, 128, 4096])
    s_wq = scr("s_wq", [2, 128, 2048])
    s_glu = scr("s_glu", [2, 128, 4096])
    s_mix = scr("s_mix", [8, 128, 3072])
    s_wo = scr("s_wo", [2, 128, 4096])
    s_ff1 = scr("s_ff1", [8, 128, 4096])
    s_ff2 = scr("s_ff2", [8, 128, 4096])
    s_pg = scr("s_pg", [2, 128, 4096])
    s_ple = scr("s_ple", [128, 2048])

    bias_cols = {}
    nbias = [0]

    def bias_col(h, j, kt):
        key = (h, j, kt)
        if key not in bias_cols:
            bias_cols[key] = nbias[0]
            nbias[0] += 1
        return bias_cols[key]

    for hp in range(2):
        for j in range(NJ):
            for hh in range(2):
                for kt in range(8 * j + 8):
                    bias_col(2 * hp + hh, j, kt)
    NBIAS = nbias[0]
    btab_d = dram_in("btab", [128, NBIAS])

    es = ExitStack()
    with es:
        def sb(name, shape, dt, stack=es):
            return stack.enter_context(nc.sbuf_tensor("t_" + name, list(shape), dt))

        banks = [es.enter_context(nc.psum_tensor("pb%d" % i, [128, 512], F32)) for i in range(8)]
        bankB = [Buf("pb%d" % i, psum=True) for i in range(8)]
        blk = es.enter_context(nc.Block())
        pe = Eng(nc, nc.tensor, "pe", True)
        act = Eng(nc, nc.scalar, "act")
        dve = Eng(nc, nc.vector, "dve")
        pool = Eng(nc, nc.gpsimd, "pool")
        sp = Eng(nc, nc.sync, "sp")
        engines = [pe, act, dve, pool, sp]
        dma_bufs = []

        def DB(name):
            b = Buf(name)
            dma_bufs.append(b)
            return b

        def barrier():
            for e in engines:
                for f in engines:
                    if f.cnt > 0:
                        e.wait(Ev(f.sem, f.cnt))
                for b in dma_bufs:
                    if b.sem is not None and b.semcnt > 0:
                        e.wait(Ev(b.sem, b.semcnt))

        rot = {"i8": 0, "i4": 0}
        inflight = set()

        def nextbank(pool8=False):
            n = 8 if pool8 else 4
            key = "i8" if pool8 else "i4"
            for _ in range(n):
                i = rot[key] % n
                rot[key] += 1
                if i not in inflight:
                    return banks[i], bankB[i]
            raise RuntimeError("no free PSUM bank")

        def MM(out, lhsT, rhs, start, stop, reads, writes, signal=True, skip=False):
            pe.op(reads, writes, lambda: nc.tensor.matmul(out, lhsT=lhsT, rhs=rhs, start=start, stop=stop,
                                                          skip_group_check=skip), signal)

        def ACT(out, in_, func, reads, writes, bias=None, scale=None, accum=None):
            kw = {}
            if bias is not None:
                kw["bias"] = bias
            if scale is not None:
                kw["scale"] = scale
            if accum is not None:
                kw["accum_out"] = accum
            act.op(reads, writes, lambda: nc.scalar.activation(out=out, in_=in_, func=func, **kw))

        def TSC(eng, out, in0, s1, s2, op0, op1, reads, writes):
            e = nc.vector if eng is dve else nc.gpsimd
            if s2 is None:
                eng.op(reads, writes, lambda: e.tensor_scalar(out=out, in0=in0, scalar1=s1, scalar2=None, op0=op0))
            else:
                eng.op(reads, writes, lambda: e.tensor_scalar(out=out, in0=in0, scalar1=s1, scalar2=s2, op0=op0, op1=op1))

        def TT(eng, out, in0, in1, op, reads, writes):
            e = nc.vector if eng is dve else nc.gpsimd
            eng.op(reads, writes, lambda: e.tensor_tensor(out=out, in0=in0, in1=in1, op=op))

        def STT(out, in0, scalar, in1, op0, op1, reads, writes, accum=None):
            if accum is None:
                dve.op(reads, writes, lambda: nc.vector.scalar_tensor_tensor(out=out, in0=in0, scalar=scalar, in1=in1, op0=op0, op1=op1))
            else:
                dve.op(reads, writes, lambda: nc.vector.scalar_tensor_tensor(out=out, in0=in0, scalar=scalar, in1=in1, op0=op0, op1=op1, accum_out=accum))

        def CP(eng, out, in_, reads, writes):
            if eng is act:
                act.op(reads, writes, lambda: nc.scalar.copy(out=out, in_=in_))
            elif eng is dve:
                dve.op(reads, writes, lambda: nc.vector.tensor_copy(out=out, in_=in_))
            else:
                pool.op(reads, writes, lambda: nc.gpsimd.tensor_copy(out=out, in_=in_))

        identf = sb("identf", [128, 128], F32)
        identb = sb("identb", [128, 128], BF16)
        onesb = sb("onesb", [128, 128], BF16)
        colp = sb("colp", [128, NCOL], F32)
        OT = sb("OT", [128, 4, NJ * TS], BF16)
        mh = sb("mh", [128, 8], F32)
        epsc = sb("epsc", [128, 1], F32)
        lam = sb("lam", [128, 4], F32)
        stt_ = sb("stt", [128, 8, 16], F32)
        xts = [sb("xt%d" % i, [128, D], F32) for i in range(2)]
        xtB = [DB("xt%d" % i) for i in range(2)]
        xhb4 = sb("xhb4", [128, 4, D], BF16)
        xhbB = [Buf("xhb%d" % i) for i in range(4)]
        hTs = [sb("hT%d" % i, [128, 8, TS], BF16) for i in range(2)]
        hTB = [[Buf("hT%d_%d" % (i, c)) for c in range(8)] for i in range(2)]
        identB, onesB, colpB, mhB, lamB, gscB = DB("ident"), Buf("ones"), DB("colp"), Buf("mh"), Buf("lam"), Buf("gsc")
        identbB = Buf("identb")
        sttB = [Buf("stt%d" % i) for i in range(8)]
        OTB = [[Buf("OT%d_%d" % (h, j)) for j in range(NJ)] for h in range(4)]
        castB = {k: DB("c_" + k) for k in ["wkv0", "wkv1", "wq0", "wq1", "rest"]}
        cnt = {"xt": 0, "st": 0, "hT": 0, "sq": 0}
        stq = [sb("stq%d" % i, [128, 4, 16], F32) for i in range(2)]
        stqB = [Buf("stq%d" % i) for i in range(2)]

        sp.dma(identf[:], ident_d[:, :], [], [identB], identB)
        sp.dma(colp[:], colp_d[:, :], [], [colpB], colpB)
        CP(dve, identb[:], identf[:], [identB], [identbB])
        pool.op([], [onesB], lambda: nc.gpsimd.memset(onesb[:], 1.0))
        pool.op([], [mhB], lambda: nc.gpsimd.memset(mh[:], -0.5))
        pool.op([], [mhB], lambda: nc.gpsimd.memset(epsc[:], EPS))
        TSC(dve, colp[:, C_CW:C_CW + 124], colp[:, C_CW:C_CW + 124], 0.5, None, ALU.mult, None, [colpB], [colpB])
        TSC(dve, colp[:, C_SG:C_SG + 4], colp[:, C_SG:C_SG + 4], 1.0 - LAMBDA_INIT, None, ALU.mult, None, [colpB], [colpB])

        def kc_view(w, c0, c1):
            return w[:, c0:c1].rearrange("(k p) n -> p k n", p=128)

        def cast(dst, src, key):
            pool.dma(dst, src, [], [castB[key]], castB[key])

        for hp in range(2):
            v = s_wkv[hp].rearrange("p (k n) -> p k n", k=8)
            cast(v[:, :, 0:256], kc_view(w_in, 1536 + hp * 256, 1536 + hp * 256 + 256), "wkv%d" % hp)
            cast(v[:, :, 256:512], kc_view(w_in, 2048 + hp * 256, 2048 + hp * 256 + 256), "wkv%d" % hp)
            cast(s_wq[hp].rearrange("p (k n) -> p k n", k=8), kc_view(w_in, 1024 + hp * 256, 1024 + hp * 256 + 256), "wq%d" % hp)
        rest_casts = []

        def cast_rest(dst, src):
            rest_casts.append((dst, src))

        for a in range(2):
            cast_rest(s_glu[a].rearrange("p (k n) -> p k n", k=8), kc_view(w_in, a * 512, a * 512 + 512))
        for m in range(8):
            cast_rest(s_mix[m][:, 0:512].rearrange("p (k n) -> p k n", k=4), kc_view(w_co, m * 128, m * 128 + 128))
            cast_rest(s_mix[m][:, 512:1024].rearrange("p (k n) -> p k n", k=4), kc_view(w_ao, m * 128, m * 128 + 128))
            cast_rest(s_mix[m][:, 1024:2048].rearrange("p (k n) -> p k n", k=8), kc_view(w_in, 2560 + m * 128, 2560 + m * 128 + 128))
            cast_rest(s_mix[m][:, 2048:3072].rearrange("p (k n) -> p k n", k=8), kc_view(w_in, 3584 + m * 128, 3584 + m * 128 + 128))
        for a in range(2):
            cast_rest(s_wo[a].rearrange("p (k n) -> p k n", k=8), kc_view(w_o, a * 512, a * 512 + 512))
        for n in range(8):
            cast_rest(s_ff1[n].rearrange("p (k n) -> p k n", k=8), kc_view(w_ff1, n * 512, n * 512 + 512))
        for half in range(2):
            for kg in range(4):
                src = w_ff2[kg * 1024:(kg + 1) * 1024, half * 512:(half + 1) * 512].rearrange("(k p) n -> p k n", p=128)
                cast_rest(s_ff2[half * 4 + kg].rearrange("p (k n) -> p k n", k=8), src)
        for a in range(2):
            cast_rest(s_pg[a].rearrange("p (k n) -> p k n", k=8), kc_view(w_pg, a * 512, a * 512 + 512))
        cast_rest(s_ple.rearrange("p (k n) -> p k n", k=2), w_ple.rearrange("(k p) n -> p k n", p=128))

        def ln_stats_a(src, srcB, P):
            slot = cnt["st"] % 8
            cnt["st"] += 1
            st = stt_[0:P, slot, :]
            B_ = sttB[slot]
            dve.op([srcB], [B_], lambda: nc.vector.bn_stats(out=st[:, 0:6], in_=src[:, 0:512]))
            dve.op([srcB], [B_], lambda: nc.vector.bn_stats(out=st[:, 6:12], in_=src[:, 512:1024]))
            dve.op([B_], [B_], lambda: nc.vector.bn_aggr(out=st[:, 12:14], in_=st[:, 0:12]))
            TSC(dve, st[:, 14:15], st[:, 13:14], EPS, None, ALU.add, None, [B_], [B_])
            TT(pool, st[:, 14:15], st[:, 14:15], mh[0:P, 0:1], ALU.pow, [B_, mhB], [B_])
            return st, B_

        def ln_stats_b(st, B_):
            STT(st[:, 15:16], st[:, 12:13], -1.0, st[:, 14:15], ALU.mult, ALU.mult, [B_], [B_])
            return st[:, 14:15], st[:, 15:16], B_

        def ln_stats(src, srcB, P):
            st, B_ = ln_stats_a(src, srcB, P)
            return ln_stats_b(st, B_)

        def transposes_to(hT, hTb, gcol, bcol, pool8):
            for c in range(8):
                bk, bkB = nextbank(pool8)
                for s in range(4):
                    MM(bk[:, s * 128:(s + 1) * 128], xhb4[:, s, c * 128:(c + 1) * 128], identb[:], True, True,
                       [xhbB[s], identbB], [bkB], signal=(s == 3))
                if c % 2 == 0:
                    TSC(dve, hT[:, c, :], bk[:, :], colp[:, gcol + c:gcol + c + 1], colp[:, bcol + c:bcol + c + 1],
                        ALU.mult, ALU.add, [bkB, colpB], [hTb[c]])
                else:
                    ACT(hT[:, c, :], bk[:, :], AF.Identity, [bkB, colpB], [hTb[c]], bias=colp[:, bcol + c:bcol + c + 1],
                        scale=colp[:, gcol + c:gcol + c + 1])

        xpool = {"bufs": [(xts[0], xtB[0]), (xts[1], xtB[1])]}

        def ln_part(src_rows, save=None):
            if len(xpool["bufs"]) >= 4:
                qi = cnt["sq"] % 2
                cnt["sq"] += 1
                st, stB = stq[qi], stqB[qi]
                xs = []
                for s in range(4):
                    xt, xtb = xpool["bufs"][cnt["xt"] % len(xpool["bufs"])]
                    cnt["xt"] += 1
                    sp.dma(xt[:], src_rows[s * 128:(s + 1) * 128, :], [], [xtb], xtb)
                    dve.op([xtb], [stB], lambda xt=xt, s=s: nc.vector.bn_stats(out=st[:, s, 0:6], in_=xt[:, 0:512]))
                    dve.op([xtb], [stB], lambda xt=xt, s=s: nc.vector.bn_stats(out=st[:, s, 6:12], in_=xt[:, 512:1024]))
                    dve.op([stB], [stB], lambda s=s: nc.vector.bn_aggr(out=st[:, s, 12:14], in_=st[:, s, 0:12]))
                    xs.append((xt, xtb))
                TSC(dve, st[:, :, 14], st[:, :, 13], EPS, None, ALU.add, None, [stB], [stB])
                TT(pool, st[:, :, 14], st[:, :, 14], mh[:, 0:4], ALU.pow, [stB, mhB], [stB])
                STT(st[:, :, 15], st[:, :, 12], -1.0, st[:, :, 14], ALU.mult, ALU.mult, [stB], [stB])
                for s in range(4):
                    xt, xtb = xs[s]
                    ACT(xhb4[:, s, :], xt[:], AF.Identity, [xtb, stB], [xhbB[s]], bias=st[:, s, 15:16], scale=st[:, s, 14:15])
                    if save is not None:
                        sv, svB = save
                        TSC(dve, sv[:, s, 0:2], st[:, s, 14:16], 1.0, None, ALU.mult, None, [stB], [svB])
                return
            for s in range(4):
                k = cnt["xt"] % 2
                cnt["xt"] += 1
                sp.dma(xts[k][:], src_rows[s * 128:(s + 1) * 128, :], [], [xtB[k]], xtB[k])
                rstd, nb, stB = ln_stats(xts[k], xtB[k], 128)
                ACT(xhb4[:, s, :], xts[k][:], AF.Identity, [xtB[k], stB], [xhbB[s]], bias=nb, scale=rstd)
                if save is not None:
                    sv, svB = save
                    TSC(dve, sv[:, s, 0:1], rstd, 1.0, None, ALU.mult, None, [stB], [svB])
                    TSC(dve, sv[:, s, 1:2], nb, 1.0, None, ALU.mult, None, [stB], [svB])

        def tr_part(pool8):
            i = cnt["hT"] % 2
            cnt["hT"] += 1
            hT, hTb = hTs[i], hTB[i]
            transposes_to(hT, hTb, C_LN0G, C_LN0B, pool8)
            return hT, hTb

        def make_hT(src_rows, pool8, save=None):
            ln_part(src_rows, save)
            return tr_part(pool8)

        with ExitStack() as ka:
            KT = sb("KT", [128, 2, S], BF16, ka)
            Vt = sb("Vt", [128, 64, 2, 128], BF16, ka)
            QTs = [sb("QT%d" % i, [128, 2, TS], BF16, ka) for i in range(2)]
            wkv = sb("wkv", [128, 8, 512], BF16, ka)
            wq = sb("wq", [128, 8, 256], BF16, ka)
            PTs = [sb("PT%d" % i, [128, TS], BF16, ka) for i in range(6)]
            masks = sb("masks", [128, 17, TS], BF16, ka)
            btab = sb("btab", [128, NBIAS], F32, ka)
            NZT = 4
            ztmp = [sb("ztmp%d" % i, [128, TS], F32, ka) for i in range(NZT)]
            dd1 = sb("dd1", [128, TS], F32, ka)
            dd = sb("dd", [128, TS], F32, ka)
            trec = sb("trec", [128, TS], F32, ka)
            trec2 = sb("trec2", [128, TS], F32, ka)
            rrt = sb("rrt", [128, TS], F32, ka)
            dsq = sb("dsq", [128, TS], BF16, ka)
            rowa = sb("rowa", [128, 768], F32, ka)
            ljunk = sb("ljunk", [128, 64], F32, ka)
            for i in range(2, 4):
                xpool["bufs"].append((sb("xt%d" % i, [128, D], F32, ka), DB("xt%d" % i)))
            KTB = [Buf("KT%d" % g) for g in range(NG)]
            VB = [Buf("V%d" % g) for g in range(NG)]
            QTB = [Buf("QT%d" % i) for i in range(2)]
            wkvB, wqB = DB("wkv"), DB("wq")
            PTB = [Buf("PT%d" % i) for i in range(6)]
            masksB, btabB, rowaB = DB("masks"), DB("btab"), DB("rowa")
            ztB = [Buf("zt%d" % i) for i in range(NZT)]
            dd1B, ddB, trecB, rrtB, dsqB, ljB = Buf("dd1"), Buf("dd"), Buf("trec"), Buf("rrt"), Buf("dsq"), Buf("lj")
            trec2B = Buf("trec2")

            sp.dma(masks[:], mask_d.rearrange("p (z q) -> p z q", z=17), [], [masksB], masksB)
            sp.dma(btab[:], btab_d[:, :], [], [btabB], btabB)
            sp.dma(rowa[:], rowa_d.partition_broadcast(128), [], [rowaB], rowaB)
            STT(ljunk[:], rowa[:, 512:576], 1.0, rowa[:, 576:640], ALU.mult, ALU.mult, [rowaB], [ljB, lamB], accum=lam[:, 0:1])
            STT(ljunk[:], rowa[:, 640:704], 1.0, rowa[:, 704:768], ALU.mult, ALU.mult, [rowaB, ljB], [ljB, lamB], accum=lam[:, 1:2])
            ACT(lam[:, 0:2], lam[:, 0:2], AF.Exp, [lamB], [lamB])
            TT(dve, lam[:, 2:3], lam[:, 0:1], lam[:, 1:2], ALU.subtract, [lamB], [lamB])
            TSC(dve, lam[:, 2:3], lam[:, 2:3], LAMBDA_INIT, None, ALU.add, None, [lamB], [lamB])
            TSC(dve, lam[:, 3:4], lam[:, 2:3], -1.0, None, ALU.mult, None, [lamB], [lamB])

            pcnt = {"pt": 0, "zt": 0}
            for hp in range(2):
                sp.dma(wkv[:], s_wkv[hp].rearrange("p (k n) -> p k n", k=8), [castB["wkv%d" % hp]], [wkvB], wkvB)
                sp.dma(wq[:], s_wq[hp].rearrange("p (k n) -> p k n", k=8), [castB["wq%d" % hp]], [wqB], wqB)
                ln_part(xf[0:TS, :])
                for g in range(NG):
                    hT, hTb = tr_part(True)
                    if g + 1 < NG:
                        ln_part(xf[(g + 1) * TS:(g + 2) * TS, :])
                    else:
                        ln_part(xo[0:TS, :])
                    for hh in range(2):
                        bk, bkB = nextbank(True)
                        for kc in range(8):
                            MM(bk[:, :], wkv[:, kc, hh * 128:(hh + 1) * 128], hT[:, kc, :], kc == 0, kc == 7,
                               [wkvB, hTb[kc]], [bkB], signal=(kc == 7))
                        CP(act, KT[:, hh, g * TS:(g + 1) * TS], bk[:, :], [bkB], [KTB[g]])
                    for s in range(4):
                        bk, bkB = nextbank(True)
                        for kc in range(8):
                            MM(bk[:, 0:256], hT[:, kc, s * 128:(s + 1) * 128], wkv[:, kc, 256:512], kc == 0, kc == 7,
                               [wkvB, hTb[kc]], [bkB], signal=(kc == 7))
                        CP(dve, Vt[:, g * 4 + s, :, 0:128], bk[:, 0:256].rearrange("p (h d) -> p h d", h=2), [bkB], [VB[g]])
                    if hp == 0:
                        pool.wait(VB[g].w)
                        for _ in range(2):
                            if rest_casts:
                                d_, s_ = rest_casts.pop(0)
                                cast(d_, s_, "rest")
                tails = []
                def q_front(jq, with_ln):
                    if with_ln:
                        ln_part(xo[jq * TS:(jq + 1) * TS, :])
                    hT, hTb = tr_part(False)
                    QT, QTb = QTs[jq % 2], QTB[jq % 2]
                    for hh in range(2):
                        bk, bkB = nextbank()
                        for kc in range(8):
                            MM(bk[:, :], wq[:, kc, hh * 128:(hh + 1) * 128], hT[:, kc, :], kc == 0, kc == 7,
                               [wqB, hTb[kc]], [bkB], signal=(kc == 7))
                        CP(act, QT[:, hh, :], bk[:, :], [bkB], [QTb])

                q_front(0, False)
                for j in range(NJ):
                    QT, QTb = QTs[j % 2], QTB[j % 2]
                    nkt = 8 * j + 8
                    steps = [(hh, kt) for hh in range(2) for kt in range(nkt)]
                    sbanks = {}

                    def emit_qk(idx):
                        hh, kt = steps[idx]
                        pair = []
                        for c in range(2):
                            bk, bkB = nextbank()
                            MM(bk[:, :], KT[64 * c:64 * c + 64, hh, kt * 128:(kt + 1) * 128], QT[64 * c:64 * c + 64, hh, :],
                               True, True, [KTB[kt // 4], QTb], [bkB], signal=(c == 1))
                            pair.append((bk, bkB))
                            inflight.add(banks.index(bk))
                        sbanks[idx] = pair

                    emit_qk(0)
                    for idx, (hh, kt) in enumerate(steps):
                        if idx + 1 < len(steps):
                            emit_qk(idx + 1)
                        h = 2 * hp + hh
                        pair = sbanks.pop(idx)
                        col = bias_col(h, j, kt)
                        pts = []
                        for c in range(2):
                            bk, bkB = pair[c]
                            pi = pcnt["pt"] % 6
                            pcnt["pt"] += 1
                            pt, ptB = PTs[pi], PTB[pi]
                            src, srcB = bk, bkB
                            mi = None
                            if kt >= 8 * j:
                                mi = (kt - 8 * j) + (9 if h == 0 else 0)
                            elif h == 0:
                                mi = 8
                            if mi is not None:
                                zi = pcnt["zt"] % NZT
                                pcnt["zt"] += 1
                                TT(dve, ztmp[zi][:], bk[:, :], masks[:, mi, :], ALU.add, [bkB, masksB], [ztB[zi]])
                                src, srcB = ztmp[zi], ztB[zi]
                            ACT(pt[:, :], src[:, :], AF.Exp, [srcB, btabB], [ptB], bias=btab[:, col:col + 1], scale=0.125)
                            pts.append((pt, ptB))
                            inflight.discard(banks.index(bk))
                        for t_ in list(tails):
                            t_[0] -= 1
                            if t_[0] <= 0:
                                tails.remove(t_)
                                t_[1]()
                        for c in range(2):
                            pt, ptB = pts[c]
                            Ob, ObB = banks[4 + 2 * c], bankB[4 + 2 * c]
                            Lb, LbB = banks[5 + 2 * c], bankB[5 + 2 * c]
                            MM(Ob[:, :], Vt[:, kt, hh, 0:128], pt[:, :], kt == 0, kt == nkt - 1, [ptB, VB[kt // 4]], [ObB], signal=False)
                            MM(Lb[:, :], onesb[:, :], pt[:, :], kt == 0, kt == nkt - 1, [ptB, onesB], [LbB], signal=True)
                        if hh == 0 and kt == nkt // 2 and j + 1 < NJ:
                            q_front(j + 1, True)
                        if kt == nkt - 1:
                            ACT(trec[:, :], banks[5][:, :], AF.Ln, [bankB[5]], [trecB])
                            ACT(trec[:, :], trec[:, :], AF.Exp, [trecB], [trecB], scale=-1.0)
                            ACT(trec2[:, :], banks[7][:, :], AF.Ln, [bankB[7]], [trec2B])
                            ACT(trec2[:, :], trec2[:, :], AF.Exp, [trec2B], [trec2B], scale=-1.0)
                            TT(dve, dd1[:, :], banks[4][:, :], trec[:, :], ALU.mult, [bankB[4], trecB], [dd1B])
                            TT(dve, trec2[:, :], banks[6][:, :], trec2[:, :], ALU.mult, [bankB[6], trec2B], [trec2B])
                            STT(dd[:, :], trec2[:, :], lam[:, 3:4], dd1[:, :], ALU.mult, ALU.add, [trec2B, lamB, dd1B], [ddB])
                            TT(dve, dsq[:, :], dd[:, :], dd[:, :], ALU.mult, [ddB], [dsqB])
                            def tail(h=h, j=j, hp=hp):
                                mb, mbB = nextbank()
                                MM(mb[:, :], onesb[:, :], dsq[:, :], True, True, [dsqB, onesB], [mbB], signal=True)
                                ACT(rrt[:, :], mb[:, :], AF.Ln, [mbB, mhB], [rrtB], bias=epsc[:, 0:1], scale=1.0 / 128.0)
                                ACT(rrt[:, :], rrt[:, :], AF.Exp, [rrtB], [rrtB], scale=-0.5)
                                STT(OT[:, h, j * TS:(j + 1) * TS], dd[:, :], colp[:, C_SG + h:C_SG + h + 1], rrt[:, :], ALU.mult, ALU.mult,
                                    [ddB, colpB, rrtB], [OTB[h][j]])
                                if hp == 0:
                                    pool.wait(OTB[h][j].w)
                                    for _ in range(2):
                                        if rest_casts:
                                            d_, s_ = rest_casts.pop(0)
                                            cast(d_, s_, "rest")
                            tails.append([3, tail])
                while tails:
                    tails.pop(0)[1]()
            while rest_casts:
                d_, s_ = rest_casts.pop(0)
                cast(d_, s_, "rest")
            if debug is not None and debug[0] == "att":
                dbgB = DB("dbg")
                dbt = sb("dbt", [128, 4, 1024], F32, ka)
                dbtB = Buf("dbt")
                for q in range(4):
                    CP(dve, dbt[:], OT[:, :, q * 1024:(q + 1) * 1024], [b for hh_ in OTB for b in hh_], [dbtB])
                    pool.dma(dbg_d[:, q * 4096:(q + 1) * 4096].rearrange("p (h t) -> p h t", h=4), dbt[:], [dbtB], [dbgB], dbgB)
                pool.wait(dbgB.w)
            barrier()
        xpool["bufs"] = xpool["bufs"][:2]

        if debug is not None and debug[0] == "att":
            pool.dma(out_d[0:128, :], xts[0][:], [xtB[0]], [castB["rest"]], castB["rest"])
            pool.wait(castB["rest"].w)
            run_all(blk, pe, act, dve, pool, sp)
            return nc, bias_cols, NBIAS

        with ExitStack() as pm:
            NSLOT = 4
            ring = [sb("ring%d" % i, [128, 4096], BF16, pm) for i in range(NSLOT)]
            ringB = [DB("ring%d" % i) for i in range(NSLOT)]
            rowp = sb("rowp", [128, 6 * D], F32, pm)
            rowpB = DB("rowp")
            Y = sb("Y", [128, 4, D], F32, pm)
            YB = [DB("Y%d" % s) for s in range(4)]
            big = sb("big", [128, 8192], F32, pm)
            hid = big[:, :].bitcast(BF16).rearrange("p (k t) -> p k t", k=32)
            uT = big[:, 0:1084].bitcast(BF16).rearrange("p (m t) -> p m t", m=4)
            uTB = [Buf("uT%d" % m) for m in range(4)]
            cbf = big[:, 1088:2112].bitcast(BF16).rearrange("p (m t) -> p m t", m=4)
            cbfB = Buf("cbf")
            csq = big[:, 2112:3136].bitcast(BF16).rearrange("p (m t) -> p m t", m=4)
            csqB = Buf("csq")
            sT = big[:, 3136:4160].bitcast(BF16).rearrange("p (m t) -> p m t", m=4)
            sTB = [Buf("sT%d" % m) for m in range(4)]
            diag = [sb("diag0", [128, 31, 128], BF16, pm)] * 2
            diagB = [Buf("diag0")] * 2
            st0 = [sb("st0_%d" % i, [128, 4, 2], F32, pm) for i in range(2)]
            st0B = [Buf("st0_%d" % i) for i in range(2)]
            mT = sb("mT", [128, 8, TS], BF16, pm)
            mTB = [Buf("mT%d" % m) for m in range(8)]
            h1T = sb("h1T", [128, 8, TS], BF16, pm)
            h1TB = [Buf("h1T_%d" % c) for c in range(8)]
            hidB = [Buf("hid%d" % k) for k in range(32)]
            smallB = uTB + [cbfB, csqB] + sTB

            def alias_fence(dsts, srcs):
                for d_ in dsts:
                    for s_ in srcs:
                        evs = list(s_.r.values()) + ([s_.w] if s_.w is not None else [])
                        for ev in evs:
                            k = id(ev.sem)
                            o = d_.r.get(k)
                            if o is None or o.val < ev.val:
                                d_.r[k] = ev
            pTt = sb("pTt", [128, 2, TS], BF16, pm)
            pTB = Buf("pTt")
            pbf = sb("pbf", [128, 4, 256], BF16, pm)
            pbfB = [Buf("pbf%d" % s) for s in range(4)]
            ftmp = [sb("ftmp%d" % i, [128, TS], F32, pm) for i in range(2)] + [big[:, 5696:6208], big[:, 6208:6720]]
            ftmpB = [Buf("ftmp%d" % i) for i in range(4)]
            rtmp = [ftmp[i][:, 0:256].bitcast(BF16) for i in range(2)]
            rtmpB = [ftmpB[i] for i in range(2)]
            hTh = sb("hTh", [128, 8, 32], BF16, pm)
            hThB = Buf("hTh")
            uh = sb("uh", [128, 64], F32, pm)
            uhB = Buf("uh")
            mstat = big[:, 4160:5696].rearrange("p (a t) -> p a t", a=3)
            mstatB = Buf("mstat")
            smallB = smallB + [mstatB, ftmpB[2], ftmpB[3]]
            fcnt = {"f": 0, "r": 0, "p": 0, "n": 4}

            def ft():
                i = fcnt["f"] % fcnt["n"]
                fcnt["f"] += 1
                return ftmp[i], ftmpB[i]

            plew = sb("plew", [128, 2, 1024], BF16, pm)
            plewB = DB("plew")
            sp.dma(plew[:], s_ple.rearrange("p (k n) -> p k n", k=2), [castB["rest"]], [plewB], plewB)
            sp.dma(rowp[:], rowp_d.partition_broadcast(128), [], [rowpB], rowpB)
            TSC(dve, rowp[:, 0:4 * D], rowp[:, 0:4 * D], ALPHA, None, ALU.mult, None, [rowpB], [rowpB])

            pieces = []
            for j in range(NJ):
                pieces.append((s_glu[0], 4096))
                pieces.append((s_glu[1], 4096))
                for m in range(8):
                    pieces.append((s_mix[m], 3072))
                pieces.append((s_wo[0], 4096))
                pieces.append((s_wo[1], 4096))
                if debug is not None and debug[0] == "pre1":
                    continue
                for n in range(8):
                    pieces.append((s_ff1[n], 4096))
                for q in range(8):
                    pieces.append((s_ff2[q], 4096))
                pieces.append((s_pg[0], 4096))
                pieces.append((s_pg[1], 4096))
            pstate = {"issued": 0, "next": 0, "rel": 0}

            def issue_to(n):
                while pstate["issued"] < min(n, len(pieces)):
                    i = pstate["issued"]
                    ap_, ncol = pieces[i]
                    sl = i % NSLOT
                    sp.dma(ring[sl][:, 0:ncol], ap_, [castB["rest"]], [ringB[sl]], ringB[sl])
                    pstate["issued"] += 1

            def next_piece():
                i = pstate["next"]
                pstate["next"] += 1
                assert i < pstate["rel"] + NSLOT
                issue_to(i + 1)
                sl = i % NSLOT
                return ring[sl], ringB[sl]

            def release(n=1):
                pstate["rel"] += n
                issue_to(pstate["rel"] + NSLOT)

            issue_to(NSLOT)

            xhbh = pbf[0:32, :, :].rearrange("p s d -> p (s d)")

            def halo_ln(j):
                k = cnt["xt"] % 2
                cnt["xt"] += 1
                xh32, xh32B = xts[k][0:32, :], xtB[k]
                sp.dma(xh32, xhalo[j * 32:(j + 1) * 32, :], [], [xh32B], xh32B)
                rstd, nb, stB = ln_stats(xh32, xh32B, 32)
                ACT(xhbh, xh32, AF.Identity, [xh32B, stB], pbfB, bias=nb, scale=rstd)

            def halo_tr(pool8):
                bk, bkB = nextbank(pool8)
                for c in range(8):
                    MM(bk[:, c * 32:(c + 1) * 32], xhbh[:, c * 128:(c + 1) * 128], identb[0:32, 0:32], True, True,
                       pbfB + [identbB], [bkB], signal=(c == 7))
                for c in range(8):
                    TSC(dve, hTh[:, c, :], bk[:, c * 32:(c + 1) * 32], colp[:, C_LN0G + c:C_LN0G + c + 1],
                        colp[:, C_LN0B + c:C_LN0B + c + 1], ALU.mult, ALU.add, [bkB, colpB], [hThB])

            DH = [(0, 16), (16, 31)]
            diagHB = [Buf("diagA"), Buf("diagB")]

            def build_diag(m, hf):
                w0, w1 = DH[hf]
                n = w1 - w0
                TT(dve, diag[0][:, w0:w1, :], identb[:, :].unsqueeze(1).broadcast_to([128, n, 128]),
                   colp[:, C_CW + m * 31 + w0:C_CW + m * 31 + w1].unsqueeze(2).broadcast_to([128, n, 128]), ALU.mult,
                   [identbB, colpB], [diagHB[hf]])

            pending = []
            nxt = make_hT(xo[0:TS, :], True, (st0[0], st0B[0]))
            halo_ln(0)
            halo_tr(True)
            for j in range(NJ):
                hT, hTb = nxt
                build_diag(0, 0)
                build_diag(0, 1)
                alias_fence(smallB, hidB)
                fcnt["n"] = 4
                wa, waB = next_piece()
                wg, wgB = next_piece()
                wa3 = wa[:, :].rearrange("p (k n) -> p k n", k=8)
                wg3 = wg[:, :].rearrange("p (k n) -> p k n", k=8)
                for m in range(4):
                    ba, baB = nextbank(True)
                    bg, bgB = nextbank(True)
                    bh, bhB = nextbank(True)
                    for kc in range(8):
                        MM(ba[:, :], wa3[:, kc, m * 128:(m + 1) * 128], hT[:, kc, :], kc == 0, kc == 7, [waB, hTb[kc]], [baB], signal=(kc == 7))
                    for kc in range(8):
                        MM(bg[:, :], wg3[:, kc, m * 128:(m + 1) * 128], hT[:, kc, :], kc == 0, kc == 7, [wgB, hTb[kc]], [bgB], signal=(kc == 7))
                    for kc in range(8):
                        MM(bh[:, 0:32], wa3[:, kc, m * 128:(m + 1) * 128], hTh[:, kc, :], kc == 0, False, [waB, hThB], [bhB], signal=False, skip=True)
                    for kc in range(8):
                        MM(bh[:, 32:64], wg3[:, kc, m * 128:(m + 1) * 128], hTh[:, kc, :], False, kc == 7, [wgB, hThB], [bhB], signal=(kc == 7), skip=True)
                    f, fB = ft()
                    ACT(f[:, :], bg[:, :], AF.Tanh, [bgB], [fB], scale=0.5)
                    STT(uT[:, m, 30:30 + TS], f[:, :], 1.0, ba[:, :], ALU.add, ALU.mult, [fB, baB], [uTB[m]])
                    ACT(uh[:, 32:64], bh[:, 32:64], AF.Tanh, [bhB], [uhB], scale=0.5)
                    STT(uh[:, 0:32], uh[:, 32:64], 1.0, bh[:, 0:32], ALU.add, ALU.mult, [uhB, bhB], [uhB])
                    TSC(dve, uT[:, m, 0:30], uh[:, 2:32], colp[:, C_HM + j:C_HM + j + 1], None, ALU.mult, None, [uhB, colpB], [uTB[m]])
                    if m % 2 == 1 and pending:
                        pending.pop(0)()
                release(2)
                cbanks = []
                for m in range(4):
                    dg = diag[0]
                    cb_, cbB_ = nextbank(True)
                    for hf in range(2):
                        w0, w1 = DH[hf]
                        for w in range(w0, w1):
                            MM(cb_[:, :], dg[:, w, :], uT[:, m, w:w + TS], w == 0, w == 30, [diagHB[hf], uTB[m]], [cbB_],
                               signal=(w == w1 - 1))
                        if m + 1 < 4:
                            build_diag(m + 1, hf)
                    cbanks.append((cb_, cbB_))
                    if m % 2 == 0 and pending:
                        pending.pop(0)()
                    if m >= 1:
                        CP(act, cbf[:, m - 1, :], cbanks[m - 1][0][:, :], [cbanks[m - 1][1]], [cbfB])
                        ACT(csq[:, m - 1, :], cbanks[m - 1][0][:, :], AF.Square, [cbanks[m - 1][1]], [csqB])
                while pending:
                    pending.pop(0)()
                CP(act, cbf[:, 3, :], cbanks[3][0][:, :], [cbanks[3][1]], [cbfB])
                ACT(csq[:, 3, :], cbanks[3][0][:, :], AF.Square, [cbanks[3][1]], [csqB])
                bm, bmB = nextbank(True)
                bq, bqB = nextbank(True)
                for m in range(4):
                    MM(bm[:, :], onesb[:], cbf[:, m, :], m == 0, m == 3, [onesB, cbfB], [bmB], signal=(m == 3))
                for m in range(4):
                    MM(bq[:, :], onesb[:], csq[:, m, :], m == 0, m == 3, [onesB, csqB], [bqB], signal=(m == 3))
                ACT(mstat[:, 0, :], bm[:, :], AF.Copy, [bmB], [mstatB], scale=1.0 / 512.0)
                TT(dve, mstat[:, 1, :], mstat[:, 0, :], mstat[:, 0, :], ALU.mult, [mstatB], [mstatB])
                STT(mstat[:, 1, :], bq[:, :], 1.0 / 512.0, mstat[:, 1, :], ALU.mult, ALU.subtract, [bqB, mstatB], [mstatB])
                ACT(mstat[:, 1, :], mstat[:, 1, :], AF.Ln, [mstatB, mhB], [mstatB], bias=epsc[:, 0:1])
                ACT(mstat[:, 1, :], mstat[:, 1, :], AF.Exp, [mstatB], [mstatB], scale=-0.5)
                TT(dve, mstat[:, 2, :], mstat[:, 0, :], mstat[:, 1, :], ALU.mult, [mstatB], [mstatB])
                for m in range(4):
                    f, fB = ft()
                    TT(dve, f[:, :], cbanks[m][0][:, :], mstat[:, 1, :], ALU.mult, [cbanks[m][1], mstatB], [fB])
                    TT(dve, f[:, :], f[:, :], mstat[:, 2, :], ALU.subtract, [fB, mstatB], [fB])
                    TSC(dve, f[:, :], f[:, :], colp[:, C_CG + m:C_CG + m + 1], colp[:, C_CB + m:C_CB + m + 1], ALU.mult, ALU.add,
                        [fB, colpB], [fB])
                    f2, f2B = ft()
                    ACT(f2[:, :], f[:, :], AF.Tanh, [fB], [f2B], scale=0.5)
                    STT(sT[:, m, :], f2[:, :], 1.0, f[:, :], ALU.add, ALU.mult, [f2B, fB], [sTB[m]])
                sv, svB = st0[j % 2], st0B[j % 2]

                def res0_piece(s, j=j, sv=sv, svB=svB):
                    k = cnt["xt"] % 2
                    cnt["xt"] += 1
                    sp.dma(xts[k][:], xo[j * TS + s * 128:j * TS + (s + 1) * 128, :], [], [xtB[k]], xtB[k])
                    ACT(Y[:, s, :], xts[k][:], AF.Identity, [xtB[k], svB], [YB[s]], bias=sv[:, s, 1:2], scale=sv[:, s, 0:1])
                    TT(dve, Y[:, s, :], Y[:, s, :], rowp[:, 0:D], ALU.mult, [YB[s], rowpB], [YB[s]])
                    TT(dve, Y[:, s, :], Y[:, s, :], rowp[:, D:2 * D], ALU.add, [YB[s], rowpB], [YB[s]])

                for m in range(8):
                    wm, wmB = next_piece()
                    byc, bycB = nextbank(True)
                    bya, byaB = nextbank(True)
                    bgc, bgcB = nextbank(True)
                    bga, bgaB = nextbank(True)
                    for kc in range(8):
                        MM(bgc[:, :], wm[:, 1024 + kc * 128:1024 + (kc + 1) * 128], hT[:, kc, :], kc == 0, kc == 7, [wmB, hTb[kc]], [bgcB], signal=(kc == 7))
                    for kc in range(8):
                        MM(bga[:, :], wm[:, 2048 + kc * 128:2048 + (kc + 1) * 128], hT[:, kc, :], kc == 0, kc == 7, [wmB, hTb[kc]], [bgaB], signal=(kc == 7))
                    for kc in range(4):
                        MM(bya[:, :], wm[:, 512 + kc * 128:512 + (kc + 1) * 128], OT[:, kc, j * TS:(j + 1) * TS], kc == 0, kc == 3,
                           [wmB, OTB[kc][j]], [byaB], signal=(kc == 3))
                    for kc in range(4):
                        MM(byc[:, :], wm[:, kc * 128:(kc + 1) * 128], sT[:, kc, :], kc == 0, kc == 3, [wmB, sTB[kc]], [bycB], signal=(kc == 3))
                    f1, f1B = ft()
                    f2, f2B = ft()
                    ACT(f1[:, :], bgc[:, :], AF.Tanh, [bgcB], [f1B], scale=0.5)
                    ACT(f2[:, :], bga[:, :], AF.Tanh, [bgaB], [f2B], scale=0.5)
                    STT(f1[:, :], f1[:, :], 1.0, byc[:, :], ALU.add, ALU.mult, [f1B, bycB], [f1B])
                    STT(f2[:, :], f2[:, :], 1.0, bya[:, :], ALU.add, ALU.mult, [f2B, byaB], [f2B])
                    STT(mT[:, m, :], f1[:, :], 0.5, f2[:, :], ALU.mult, ALU.add, [f1B, f2B], [mTB[m]])
                    release()
                    if 1 <= m <= 4:
                        while pending:
                            pending.pop(0)()
                        res0_piece(m - 1)
                wos = [next_piece(), next_piece()]
                ln1 = []
                prev_ln = None

                def ln1_finish(p_):
                    s_, (st_, B_) = p_
                    rstd, nb, stB = ln_stats_b(st_, B_)
                    ACT(xhb4[:, s_, :], Y[:, s_, :], AF.Identity, [YB[s_], stB], [xhbB[s_]], bias=nb, scale=rstd)
                    ln1.append((rstd, nb, stB))
                for s in range(4):
                    for half in range(2):
                        wo_, woB = wos[half]
                        wo3 = wo_[:, :].rearrange("p (k n) -> p k n", k=8)
                        bk, bkB = nextbank(True)
                        for kc in range(8):
                            MM(bk[:, :], mT[:, kc, s * 128:(s + 1) * 128], wo3[:, kc, :], kc == 0, kc == 7, [woB, mTB[kc]], [bkB], signal=(kc == 7))
                        STT(Y[:, s, half * 512:(half + 1) * 512], bk[:, :], 0.5, Y[:, s, half * 512:(half + 1) * 512], ALU.mult, ALU.add,
                            [bkB, YB[s]], [YB[s]])
                    cur = (s, ln_stats_a(Y[:, s, :], YB[s], 128))
                    if prev_ln is not None:
                        ln1_finish(prev_ln)
                    prev_ln = cur
                ln1_finish(prev_ln)
                release(2)
                transposes_to(h1T, h1TB, C_LN1G, C_LN1B, True)
                for s in range(4):
                    rstd, nb, stB = ln1[s]
                    ACT(Y[:, s, :], Y[:, s, :], AF.Identity, [YB[s], stB], [YB[s]], bias=nb, scale=rstd)
                    TT(dve, Y[:, s, :], Y[:, s, :], rowp[:, 2 * D:3 * D], ALU.mult, [YB[s], rowpB], [YB[s]])
                    TT(dve, Y[:, s, :], Y[:, s, :], rowp[:, 3 * D:4 * D], ALU.add, [YB[s], rowpB], [YB[s]])
                alias_fence(hidB, smallB)
                fcnt["n"] = 2
                for n in range(8):
                    w1, w1B = next_piece()
                    w13 = w1[:, :].rearrange("p (k n) -> p k n", k=8)
                    for i in range(4):
                        bk, bkB = nextbank()
                        for kc in range(8):
                            MM(bk[:, :], w13[:, kc, i * 128:(i + 1) * 128], h1T[:, kc, :], kc == 0, kc == 7, [w1B, h1TB[kc]], [bkB], signal=(kc == 7))
                        ri = fcnt["r"] % 2
                        fcnt["r"] += 1
                        ACT(rtmp[ri][:, :], bk[:, :], AF.Relu, [bkB], [rtmpB[ri]])
                        k_ = n * 4 + i
                        TT(dve, hid[:, k_, :], rtmp[ri][:, :], rtmp[ri][:, :], ALU.mult, [rtmpB[ri]], [hidB[k_]])
                    release()
                if j + 1 < NJ:
                    ln_part(xo[(j + 1) * TS:(j + 2) * TS, :], (st0[(j + 1) % 2], st0B[(j + 1) % 2]))
                    halo_ln(j + 1)
                for half in range(2):
                    for kg in range(4):
                        w2, w2B = next_piece()
                        w23 = w2[:, :].rearrange("p (k n) -> p k n", k=8)
                        for s in range(4):
                            for k in range(8):
                                MM(banks[4 + s][:, :], hid[:, kg * 8 + k, s * 128:(s + 1) * 128], w23[:, k, :],
                                   (kg == 0 and k == 0), (kg == 3 and k == 7), [w2B, hidB[kg * 8 + k]], [bankB[4 + s]], signal=(k == 7))
                        release()
                    for s in range(4):
                        TT(dve, Y[:, s, half * 512:(half + 1) * 512], banks[4 + s][:, :], Y[:, s, half * 512:(half + 1) * 512], ALU.add,
                           [bankB[4 + s], YB[s]], [YB[s]])
                if j + 1 < NJ:
                    nxt = tr_part(False)
                    halo_tr(False)
                for s in range(4):
                    pi = cnt["xt"] % 2
                    cnt["xt"] += 1
                    sp.dma(xts[pi][:, 0:256], po[j * TS + s * 128:j * TS + (s + 1) * 128, :], [], [xtB[pi]], xtB[pi])
                    CP(act, pbf[:, s, :], xts[pi][:, 0:256], [xtB[pi]], [pbfB[s]])
                for c2 in range(2):
                    bk, bkB = nextbank()
                    for s in range(4):
                        MM(bk[:, s * 128:(s + 1) * 128], pbf[:, s, c2 * 128:(c2 + 1) * 128], identb[:], True, True, [pbfB[s], identbB], [bkB], signal=(s == 3))
                    CP(dve, pTt[:, c2, :], bk[:, :], [bkB], [pTB])
                wpl3, wplB = plew, plewB
                for half in range(2):
                    wpg, wpgB = next_piece()
                    wpg3 = wpg[:, :].rearrange("p (k n) -> p k n", k=8)
                    for s in range(4):
                        bgt, bgtB = nextbank()
                        bpl, bplB = nextbank()
                        for kc in range(8):
                            MM(bgt[:, :], h1T[:, kc, s * 128:(s + 1) * 128], wpg3[:, kc, :], kc == 0, kc == 7, [wpgB, h1TB[kc]], [bgtB], signal=(kc == 7))
                        for c2 in range(2):
                            MM(bpl[:, :], pTt[:, c2, s * 128:(s + 1) * 128], wpl3[:, c2, half * 512:(half + 1) * 512], c2 == 0, c2 == 1,
                               [wplB, pTB], [bplB], signal=(c2 == 1))
                        f, fB = ft()
                        ACT(f[:, :], bgt[:, :], AF.Tanh, [bgtB], [fB], scale=0.5)
                        STT(f[:, :], f[:, :], 1.0, bpl[:, :], ALU.add, ALU.mult, [fB, bplB], [fB])
                        STT(Y[:, s, half * 512:(half + 1) * 512], f[:, :], 0.5, Y[:, s, half * 512:(half + 1) * 512], ALU.mult, ALU.add,
                            [fB, YB[s]], [YB[s]])
                release(2)
                def ln2_store(s, j=j):
                    rstd, nb, stB = ln_stats(Y[:, s, :], YB[s], 128)
                    ACT(Y[:, s, :], Y[:, s, :], AF.Identity, [YB[s], stB], [YB[s]], bias=nb, scale=rstd)
                    TT(dve, Y[:, s, :], Y[:, s, :], rowp[:, 4 * D:5 * D], ALU.mult, [YB[s], rowpB], [YB[s]])
                    TT(dve, Y[:, s, :], Y[:, s, :], rowp[:, 5 * D:6 * D], ALU.add, [YB[s], rowpB], [YB[s]])
                    pool.dma(out_d[j * TS + s * 128:j * TS + (s + 1) * 128, :], Y[:, s, :], [YB[s]], [], YB[s])
                for s in range(4):
                    pending.append(lambda s=s, f=ln2_store: f(s))
            while pending:
                pending.pop(0)()
            for s in range(4):
                pool.wait(Ev(YB[s].sem, YB[s].semcnt))
            barrier()
        run_all(blk, pe, act, dve, pool, sp)
    return nc, bias_cols, NBIAS


def run_all(blk, pe, act, dve, pool, sp):
    @blk.sync
    def _(e):
        sp.replay(e)

    @blk.tensor
    def _(e):
        pe.replay(e)

    @blk.scalar
    def _(e):
        act.replay(e)

    @blk.vector
    def _(e):
        dve.replay(e)

    @blk.gpsimd
    def _(e):
        pool.replay(e)


def _core_tables(r, bias_cols, NBIAS):
    p = np.arange(128, dtype=np.float64)
    btab = np.zeros((128, NBIAS), np.float32)
    for (h, j, kt), col in bias_cols.items():
        g = 2 * j + r
        kpos = kt * 128 + p
        if r == 0 and kt >= 8 * j + 4:
            btab[:, col] = NEGB
        else:
            btab[:, col] = SLOPES[h] * (kpos - (g * TS + 256))
    ki = np.arange(128)[:, None]
    qi = np.arange(TS)[None, :]
    masks = np.zeros((128, 17, TS), np.float32)
    shift = np.broadcast_to(-8.0 * SLOPES[0] * (qi - 256), (128, TS))
    masks[:, 8, :] = shift
    for u in range(4):
        causal = np.where(128 * u + ki <= qi, 0.0, NEGM)
        z = u if r == 0 else 4 + u
        masks[:, z, :] = causal
    for z in range(8):
        masks[:, 9 + z, :] = masks[:, z, :] + shift
    return btab, masks.reshape(128, 17 * TS).astype(ml_dtypes.bfloat16)


def _prep_inputs(inp, bias_cols, NBIAS):
    f = lambda a: np.ascontiguousarray(np.asarray(a, dtype=np.float32))
    x = f(inp["x"])
    p = f(inp["p"])[0]
    colT = lambda v: f(v).reshape(-1, 128).T
    conv_w = f(inp["conv_w"])[0]
    cw = np.concatenate([conv_w[:, m * 128:(m + 1) * 128].T for m in range(4)], axis=1)
    rowp = np.concatenate([f(inp["ln0_g"]), f(inp["ln0_b"]), f(inp["ln1_g"])[0], f(inp["ln1_b"])[0],
                           f(inp["ln2_g"])[0], f(inp["ln2_b"])[0]])
    rowa = np.concatenate([f(inp["subln_g"])[0], f(inp["lambda_q1"])[0], f(inp["lambda_k1"])[0],
                           f(inp["lambda_q2"])[0], f(inp["lambda_k2"])[0]])
    shared = {
        "w_in": f(inp["w_in"])[0], "w_co": f(inp["w_conv_out"])[0], "w_ao": f(inp["w_attn_out"])[0],
        "w_o": f(inp["w_o"])[0], "w_ff1": f(inp["w_ff1"])[0], "w_ff2": f(inp["w_ff2"])[0],
        "w_ple": f(inp["w_ple"])[0], "w_pg": f(inp["w_ple_gate"])[0],
        "rowp": f(rowp), "rowa": f(rowa), "ident": np.eye(128, dtype=np.float32),
    }
    in_maps = []
    for c in range(8):
        b, r = c // 2, c % 2
        xb = x[b]
        tiles = xb.reshape(NG, TS, D)
        own = [2 * j + r for j in range(NJ)]
        xo = np.ascontiguousarray(tiles[own].reshape(NJ * TS, D))
        halo = np.zeros((NJ, 32, D), np.float32)
        hm = np.zeros(NJ, np.float32)
        for j, g in enumerate(own):
            if g > 0:
                halo[j] = xb[g * TS - 32:g * TS]
                hm[j] = 1.0
        colp = np.concatenate([colT(inp["ln0_g"]), colT(inp["ln0_b"]), colT(f(inp["ln1_g"])[0]), colT(f(inp["ln1_b"])[0]),
                               colT(f(inp["conv_ln_g"])[0]), colT(f(inp["conv_ln_b"])[0]), cw,
                               np.tile(hm[None, :], (128, 1)), colT(f(inp["subln_g"])[0])], axis=1)
        assert colp.shape == (128, NCOL)
        btab, masks = _core_tables(r, bias_cols, NBIAS)
        m = dict(shared)
        m.update({"xf": np.ascontiguousarray(xb), "xo": xo, "xhalo": np.ascontiguousarray(halo.reshape(NJ * 32, D)),
                  "po": np.ascontiguousarray(p[b].reshape(NG, TS, 256)[own].reshape(NJ * TS, 256)),
                  "colp": f(colp), "btab": btab, "masks": masks})
        in_maps.append(m)
    return in_maps


_CACHE = {}


def kernel(**inputs):
    if "prog" not in _CACHE:
        _CACHE["prog"] = build_program(DEBUG)
    nc, bias_cols, NBIAS = _CACHE["prog"]
    in_maps = _prep_inputs(inputs, bias_cols, NBIAS)
    res = run_bass_kernel_spmd(nc, in_maps, core_ids=list(range(8)))
    out = np.zeros((4, S, D), np.float32)
    for c in range(8):
        b, r = c // 2, c % 2
        o = np.asarray(res.results[c]["out"]).reshape(NJ, TS, D)
        for j in range(NJ):
            g = 2 * j + r
            out[b, g * TS:(g + 1) * TS] = o[j]
    if DEBUG is not None:
        _CACHE["dbg"] = [np.asarray(r_.get("dbg")) if "dbg" in r_ else None for r_ in res.results]
    return out
```

```python
import numpy as np
import ml_dtypes
from contextlib import ExitStack
import concourse.bass as bass
import concourse.mybir as mybir
from concourse.bass_utils import run_bass_kernel_spmd

F32 = mybir.dt.float32
BF16 = mybir.dt.bfloat16
AF = mybir.ActivationFunctionType
ALU = mybir.AluOpType

D = 1024
S = 8192
TS = 512
NJ = 8
NG = 16
ALPHA = 2.0 ** 0.25
EPS = 1e-5
SLOPES = [2.0 ** (-2.0 * (h + 1)) for h in range(4)]
NEGM = -240000.0
NEGB = -30000.0
LAMBDA_INIT = 0.2

C_LN0G, C_LN0B, C_LN1G, C_LN1B, C_CG, C_CB, C_CW, C_HM, C_SG = 0, 8, 16, 24, 32, 36, 40, 164, 172
NCOL = 176
SAME_ENGINE_SYNC = True
DEBUG = None


class Ev:
    __slots__ = ("sem", "val")

    def __init__(self, sem, val):
        self.sem = sem
        self.val = val


class Buf:
    def __init__(self, name, psum=False):
        self.name = name
        self.w = None
        self.r = {}
        self.sem = None
        self.semcnt = 0
        self.psum = psum


class Eng:
    def __init__(self, nc, eng, name, is_pe=False):
        self.nc = nc
        self.e = eng
        self.name = name
        self.is_pe = is_pe
        self.sem = nc.alloc_semaphore(name="s_" + name)
        self.cnt = 0
        self.waited = {}
        self.q = []

    def wait(self, ev):
        if ev is None:
            return
        if ev.sem is self.sem and (self.is_pe or not SAME_ENGINE_SYNC):
            return
        k = id(ev.sem)
        if self.waited.get(k, 0) >= ev.val:
            return
        self.q.append((0, ev.sem, ev.val))
        self.waited[k] = ev.val

    def deps(self, reads, writes):
        for b in reads:
            self.wait(b.w)
            if b.psum:
                for e in list(b.r.values()):
                    self.wait(e)
        for b in writes:
            self.wait(b.w)
            for e in list(b.r.values()):
                self.wait(e)

    def mark(self, reads, writes, ev):
        k = id(ev.sem)
        for b in reads:
            o = b.r.get(k)
            if o is None or o.val < ev.val:
                b.r[k] = ev
        for b in writes:
            b.w = ev
            b.r = {}

    def op(self, reads, writes, fn, signal=True):
        self.deps(reads, writes)
        if signal:
            self.cnt += 1
            ev = Ev(self.sem, self.cnt)
            self.q.append((1, fn, self.sem, 1))
        else:
            ev = Ev(self.sem, self.cnt + 1)
            self.q.append((1, fn, None, 0))
        self.mark(reads, writes, ev)

    def dma(self, out, in_, reads, writes, sembuf, **kw):
        self.deps(reads, writes)
        if sembuf.sem is None:
            sembuf.sem = self.nc.alloc_semaphore(name="d_" + sembuf.name)
        e = self.e
        self.q.append((1, (lambda: e.dma_start(out=out, in_=in_, **kw)), sembuf.sem, 16))
        sembuf.semcnt += 16
        ev = Ev(sembuf.sem, sembuf.semcnt)
        self.mark(reads, writes, ev)

    def replay(self, eng):
        for it in self.q:
            if it[0] == 0:
                eng.wait_ge(it[1], it[2])
            else:
                inst = it[1]()
                if it[2] is not None:
                    inst.then_inc(it[2], it[3])


def build_program(debug=None):
    nc = bass.Bass("TRN2", target_bir_lowering=False)

    def dram_in(name, shape, dt=F32):
        return nc.dram_tensor(name, list(shape), dt, kind="ExternalInput").ap()

    xf = dram_in("xf", [S, D])
    xo = dram_in("xo", [NJ * TS, D])
    xhalo = dram_in("xhalo", [NJ * 32, D])
    po = dram_in("po", [NJ * TS, 256])
    w_in = dram_in("w_in", [D, 4608])
    w_co = dram_in("w_co", [512, D])
    w_ao = dram_in("w_ao", [512, D])
    w_o = dram_in("w_o", [D, D])
    w_ff1 = dram_in("w_ff1", [D, 4096])
    w_ff2 = dram_in("w_ff2", [4096, D])
    w_ple = dram_in("w_ple", [256, D])
    w_pg = dram_in("w_pg", [D, D])
    colp_d = dram_in("colp", [128, NCOL])
    rowp_d = dram_in("rowp", [6 * D])
    rowa_d = dram_in("rowa", [768])
    ident_d = dram_in("ident", [128, 128])
    mask_d = dram_in("masks", [128, 17 * TS], BF16)
    out_d = nc.dram_tensor("out", [NJ * TS, D], F32, kind="ExternalOutput").ap()
    dbg_d = None
    if debug is not None:
        dbg_d = nc.dram_tensor("dbg", list(debug[1]), F32, kind="ExternalOutput").ap()

    def scr(name, shape):
        return nc.dram_tensor(name, list(shape), BF16, kind="Internal").ap()

    s_wkv = scr("s_wkv", [2, 128, 4096])
    s_wq = scr("s_wq", [2, 128, 2048])
    s_glu = scr("s_glu", [2, 128, 4096])
    s_mix = scr("s_mix", [8, 128, 3072])
    s_wo = scr("s_wo", [2, 128, 4096])
    s_ff1 = scr("s_ff1", [8, 128, 4096])
    s_ff2 = scr("s_ff2", [8, 128, 4096])
    s_pg = scr("s_pg", [2, 128, 4096])
    s_ple = scr("s_ple", [128, 2048])

    bias_cols = {}
    nbias = [0]

    def bias_col(h, j, kt):
        key = (h, j, kt)
        if key not in bias_cols:
            bias_cols[key] = nbias[0]
            nbias[0] += 1
        return bias_cols[key]

    for hp in range(2):
        for j in range(NJ):
            for hh in range(2):
                for kt in range(8 * j + 8):
                    bias_col(2 * hp + hh, j, kt)
    NBIAS = nbias[0]
    btab_d = dram_in("btab", [128, NBIAS])

    es = ExitStack()
    with es:
        def sb(name, shape, dt, stack=es):
            return stack.enter_context(nc.sbuf_tensor("t_" + name, list(shape), dt))

        banks = [es.enter_context(nc.psum_tensor("pb%d" % i, [128, 512], F32)) for i in range(8)]
        bankB = [Buf("pb%d" % i, psum=True) for i in range(8)]
        blk = es.enter_context(nc.Block())
        pe = Eng(nc, nc.tensor, "pe", True)
        act = Eng(nc, nc.scalar, "act")
        dve = Eng(nc, nc.vector, "dve")
        pool = Eng(nc, nc.gpsimd, "pool")
        sp = Eng(nc, nc.sync, "sp")
        engines = [pe, act, dve, pool, sp]
        dma_bufs = []

        def DB(name):
            b = Buf(name)
            dma_bufs.append(b)
            return b

        def barrier():
            for e in engines:
                for f in engines:
                    if f.cnt > 0:
                        e.wait(Ev(f.sem, f.cnt))
                for b in dma_bufs:
                    if b.sem is not None and b.semcnt > 0:
                        e.wait(Ev(b.sem, b.semcnt))

        rot = {"i8": 0, "i4": 0}
        inflight = set()

        def nextbank(pool8=False):
            n = 8 if pool8 else 4
            key = "i8" if pool8 else "i4"
            for _ in range(n):
                i = rot[key] % n
                rot[key] += 1
                if i not in inflight:
                    return banks[i], bankB[i]
            raise RuntimeError("no free PSUM bank")

        def MM(out, lhsT, rhs, start, stop, reads, writes, signal=True, skip=False):
            pe.op(reads, writes, lambda: nc.tensor.matmul(out, lhsT=lhsT, rhs=rhs, start=start, stop=stop,
                                                          skip_group_check=skip), signal)

        def ACT(out, in_, func, reads, writes, bias=None, scale=None, accum=None):
            kw = {}
            if bias is not None:
                kw["bias"] = bias
            if scale is not None:
                kw["scale"] = scale
            if accum is not None:
                kw["accum_out"] = accum
            act.op(reads, writes, lambda: nc.scalar.activation(out=out, in_=in_, func=func, **kw))

        def TSC(eng, out, in0, s1, s2, op0, op1, reads, writes):
            e = nc.vector if eng is dve else nc.gpsimd
            if s2 is None:
                eng.op(reads, writes, lambda: e.tensor_scalar(out=out, in0=in0, scalar1=s1, scalar2=None, op0=op0))
            else:
                eng.op(reads, writes, lambda: e.tensor_scalar(out=out, in0=in0, scalar1=s1, scalar2=s2, op0=op0, op1=op1))

        def TT(eng, out, in0, in1, op, reads, writes):
            e = nc.vector if eng is dve else nc.gpsimd
            eng.op(reads, writes, lambda: e.tensor_tensor(out=out, in0=in0, in1=in1, op=op))

        def STT(out, in0, scalar, in1, op0, op1, reads, writes, accum=None):
            if accum is None:
                dve.op(reads, writes, lambda: nc.vector.scalar_tensor_tensor(out=out, in0=in0, scalar=scalar, in1=in1, op0=op0, op1=op1))
            else:
                dve.op(reads, writes, lambda: nc.vector.scalar_tensor_tensor(out=out, in0=in0, scalar=scalar, in1=in1, op0=op0, op1=op1, accum_out=accum))

        def CP(eng, out, in_, reads, writes):
            if eng is act:
                act.op(reads, writes, lambda: nc.scalar.copy(out=out, in_=in_))
            elif eng is dve:
                dve.op(reads, writes, lambda: nc.vector.tensor_copy(out=out, in_=in_))
            else:
                pool.op(reads, writes, lambda: nc.gpsimd.tensor_copy(out=out, in_=in_))

        identf = sb("identf", [128, 128], F32)
        identb = sb("identb", [128, 128], BF16)
        onesb = sb("onesb", [128, 128], BF16)
        colp = sb("colp", [128, NCOL], F32)
        OT = sb("OT", [128, 4, NJ * TS], BF16)
        mh = sb("mh", [128, 8], F32)
        epsc = sb("epsc", [128, 1], F32)
        lam = sb("lam", [128, 4], F32)
        stt_ = sb("stt", [128, 8, 16], F32)
        xts = [sb("xt%d" % i, [128, D], F32) for i in range(2)]
        xtB = [DB("xt%d" % i) for i in range(2)]
        xhb4 = sb("xhb4", [128, 4, D], BF16)
        xhbB = [Buf("xhb%d" % i) for i in range(4)]
        hTs = [sb("hT%d" % i, [128, 8, TS], BF16) for i in range(2)]
        hTB = [[Buf("hT%d_%d" % (i, c)) for c in range(8)] for i in range(2)]
        identB, onesB, colpB, mhB, lamB, gscB = DB("ident"), Buf("ones"), DB("colp"), Buf("mh"), Buf("lam"), Buf("gsc")
        identbB = Buf("identb")
        sttB = [Buf("stt%d" % i) for i in range(8)]
        OTB = [[Buf("OT%d_%d" % (h, j)) for j in range(NJ)] for h in range(4)]
        castB = {k: DB("c_" + k) for k in ["wkv0", "wkv1", "wq0", "wq1", "rest"]}
        cnt = {"xt": 0, "st": 0, "hT": 0, "sq": 0}
        stq = [sb("stq%d" % i, [128, 4, 16], F32) for i in range(2)]
        stqB = [Buf("stq%d" % i) for i in range(2)]

        sp.dma(identf[:], ident_d[:, :], [], [identB], identB)
        sp.dma(colp[:], colp_d[:, :], [], [colpB], colpB)
        CP(dve, identb[:], identf[:], [identB], [identbB])
        pool.op([], [onesB], lambda: nc.gpsimd.memset(onesb[:], 1.0))
        pool.op([], [mhB], lambda: nc.gpsimd.memset(mh[:], -0.5))
        pool.op([], [mhB], lambda: nc.gpsimd.memset(epsc[:], EPS))
        TSC(dve, colp[:, C_CW:C_CW + 124], colp[:, C_CW:C_CW + 124], 0.5, None, ALU.mult, None, [colpB], [colpB])
        TSC(dve, colp[:, C_SG:C_SG + 4], colp[:, C_SG:C_SG + 4], 1.0 - LAMBDA_INIT, None, ALU.mult, None, [colpB], [colpB])

        def kc_view(w, c0, c1):
            return w[:, c0:c1].rearrange("(k p) n -> p k n", p=128)

        def cast(dst, src, key):
            pool.dma(dst, src, [], [castB[key]], castB[key])

        for hp in range(2):
            v = s_wkv[hp].rearrange("p (k n) -> p k n", k=8)
            cast(v[:, :, 0:256], kc_view(w_in, 1536 + hp * 256, 1536 + hp * 256 + 256), "wkv%d" % hp)
            cast(v[:, :, 256:512], kc_view(w_in, 2048 + hp * 256, 2048 + hp * 256 + 256), "wkv%d" % hp)
            cast(s_wq[hp].rearrange("p (k n) -> p k n", k=8), kc_view(w_in, 1024 + hp * 256, 1024 + hp * 256 + 256), "wq%d" % hp)
        rest_casts = []

        def cast_rest(dst, src):
            rest_casts.append((dst, src))

        for a in range(2):
            cast_rest(s_glu[a].rearrange("p (k n) -> p k n", k=8), kc_view(w_in, a * 512, a * 512 + 512))
        for m in range(8):
            cast_rest(s_mix[m][:, 0:512].rearrange("p (k n) -> p k n", k=4), kc_view(w_co, m * 128, m * 128 + 128))
            cast_rest(s_mix[m][:, 512:1024].rearrange("p (k n) -> p k n", k=4), kc_view(w_ao, m * 128, m * 128 + 128))
            cast_rest(s_mix[m][:, 1024:2048].rearrange("p (k n) -> p k n", k=8), kc_view(w_in, 2560 + m * 128, 2560 + m * 128 + 128))
            cast_rest(s_mix[m][:, 2048:3072].rearrange("p (k n) -> p k n", k=8), kc_view(w_in, 3584 + m * 128, 3584 + m * 128 + 128))
        for a in range(2):
            cast_rest(s_wo[a].rearrange("p (k n) -> p k n", k=8), kc_view(w_o, a * 512, a * 512 + 512))
        for n in range(8):
            cast_rest(s_ff1[n].rearrange("p (k n) -> p k n", k=8), kc_view(w_ff1, n * 512, n * 512 + 512))
        for half in range(2):
            for kg in range(4):
                src = w_ff2[kg * 1024:(kg + 1) * 1024, half * 512:(half + 1) * 512].rearrange("(k p) n -> p k n", p=128)
                cast_rest(s_ff2[half * 4 + kg].rearrange("p (k n) -> p k n", k=8), src)
        for a in range(2):
            cast_rest(s_pg[a].rearrange("p (k n) -> p k n", k=8), kc_view(w_pg, a * 512, a * 512 + 512))
        cast_rest(s_ple.rearrange("p (k n) -> p k n", k=2), w_ple.rearrange("(k p) n -> p k n", p=128))

        def ln_stats_a(src, srcB, P):
            slot = cnt["st"] % 8
            cnt["st"] += 1
            st = stt_[0:P, slot, :]
            B_ = sttB[slot]
            dve.op([srcB], [B_], lambda: nc.vector.bn_stats(out=st[:, 0:6], in_=src[:, 0:512]))
            dve.op([srcB], [B_], lambda: nc.vector.bn_stats(out=st[:, 6:12], in_=src[:, 512:1024]))
            dve.op([B_], [B_], lambda: nc.vector.bn_aggr(out=st[:, 12:14], in_=st[:, 0:12]))
            TSC(dve, st[:, 14:15], st[:, 13:14], EPS, None, ALU.add, None, [B_], [B_])
            TT(pool, st[:, 14:15], st[:, 14:15], mh[0:P, 0:1], ALU.pow, [B_, mhB], [B_])
            return st, B_

        def ln_stats_b(st, B_):
            STT(st[:, 15:16], st[:, 12:13], -1.0, st[:, 14:15], ALU.mult, ALU.mult, [B_], [B_])
            return st[:, 14:15], st[:, 15:16], B_

        def ln_stats(src, srcB, P):
            st, B_ = ln_stats_a(src, srcB, P)
            return ln_stats_b(st, B_)

        def transposes_to(hT, hTb, gcol, bcol, pool8):
            for c in range(8):
                bk, bkB = nextbank(pool8)
                for s in range(4):
                    MM(bk[:, s * 128:(s + 1) * 128], xhb4[:, s, c * 128:(c + 1) * 128], identb[:], True, True,
                       [xhbB[s], identbB], [bkB], signal=(s == 3))
                if c % 2 == 0:
                    TSC(dve, hT[:, c, :], bk[:, :], colp[:, gcol + c:gcol + c + 1], colp[:, bcol + c:bcol + c + 1],
                        ALU.mult, ALU.add, [bkB, colpB], [hTb[c]])
                else:
                    ACT(hT[:, c, :], bk[:, :], AF.Identity, [bkB, colpB], [hTb[c]], bias=colp[:, bcol + c:bcol + c + 1],
                        scale=colp[:, gcol + c:gcol + c + 1])

        xpool = {"bufs": [(xts[0], xtB[0]), (xts[1], xtB[1])]}

        def ln_part(src_rows, save=None):
            if len(xpool["bufs"]) >= 4:
                qi = cnt["sq"] % 2
                cnt["sq"] += 1
                st, stB = stq[qi], stqB[qi]
                xs = []
                for s in range(4):
                    xt, xtb = xpool["bufs"][cnt["xt"] % len(xpool["bufs"])]
                    cnt["xt"] += 1
                    sp.dma(xt[:], src_rows[s * 128:(s + 1) * 128, :], [], [xtb], xtb)
                    dve.op([xtb], [stB], lambda xt=xt, s=s: nc.vector.bn_stats(out=st[:, s, 0:6], in_=xt[:, 0:512]))
                    dve.op([xtb], [stB], lambda xt=xt, s=s: nc.vector.bn_stats(out=st[:, s, 6:12], in_=xt[:, 512:1024]))
                    dve.op([stB], [stB], lambda s=s: nc.vector.bn_aggr(out=st[:, s, 12:14], in_=st[:, s, 0:12]))
                    xs.append((xt, xtb))
                TSC(dve, st[:, :, 14], st[:, :, 13], EPS, None, ALU.add, None, [stB], [stB])
                TT(pool, st[:, :, 14], st[:, :, 14], mh[:, 0:4], ALU.pow, [stB, mhB], [stB])
                STT(st[:, :, 15], st[:, :, 12], -1.0, st[:, :, 14], ALU.mult, ALU.mult, [stB], [stB])
                for s in range(4):
                    xt, xtb = xs[s]
                    ACT(xhb4[:, s, :], xt[:], AF.Identity, [xtb, stB], [xhbB[s]], bias=st[:, s, 15:16], scale=st[:, s, 14:15])
                    if save is not None:
                        sv, svB = save
                        TSC(dve, sv[:, s, 0:2], st[:, s, 14:16], 1.0, None, ALU.mult, None, [stB], [svB])
                return
            for s in range(4):
                k = cnt["xt"] % 2
                cnt["xt"] += 1
                sp.dma(xts[k][:], src_rows[s * 128:(s + 1) * 128, :], [], [xtB[k]], xtB[k])
                rstd, nb, stB = ln_stats(xts[k], xtB[k], 128)
                ACT(xhb4[:, s, :], xts[k][:], AF.Identity, [xtB[k], stB], [xhbB[s]], bias=nb, scale=rstd)
                if save is not None:
                    sv, svB = save
                    TSC(dve, sv[:, s, 0:1], rstd, 1.0, None, ALU.mult, None, [stB], [svB])
                    TSC(dve, sv[:, s, 1:2], nb, 1.0, None, ALU.mult, None, [stB], [svB])

        def tr_part(pool8):
            i = cnt["hT"] % 2
            cnt["hT"] += 1
            hT, hTb = hTs[i], hTB[i]
            transposes_to(hT, hTb, C_LN0G, C_LN0B, pool8)
            return hT, hTb

        def make_hT(src_rows, pool8, save=None):
            ln_part(src_rows, save)
            return tr_part(pool8)

        with ExitStack() as ka:
            KT = sb("KT", [128, 2, S], BF16, ka)
            Vt = sb("Vt", [128, 64, 2, 128], BF16, ka)
            QTs = [sb("QT%d" % i, [128, 2, TS], BF16, ka) for i in range(2)]
            wkv = sb("wkv", [128, 8, 512], BF16, ka)
            wq = sb("wq", [128, 8, 256], BF16, ka)
            PTs = [sb("PT%d" % i, [128, TS], BF16, ka) for i in range(6)]
            masks = sb("masks", [128, 17, TS], BF16, ka)
            btab = sb("btab", [128, NBIAS], F32, ka)
            NZT = 4
            ztmp = [sb("ztmp%d" % i, [128, TS], F32, ka) for i in range(NZT)]
            dd1 = sb("dd1", [128, TS], F32, ka)
            dd = sb("dd", [128, TS], F32, ka)
            trec = sb("trec", [128, TS], F32, ka)
            trec2 = sb("trec2", [128, TS], F32, ka)
            rrt = sb("rrt", [128, TS], F32, ka)
            dsq = sb("dsq", [128, TS], BF16, ka)
            rowa = sb("rowa", [128, 768], F32, ka)
            ljunk = sb("ljunk", [128, 64], F32, ka)
            for i in range(2, 4):
                xpool["bufs"].append((sb("xt%d" % i, [128, D], F32, ka), DB("xt%d" % i)))
            KTB = [Buf("KT%d" % g) for g in range(NG)]
            VB = [Buf("V%d" % g) for g in range(NG)]
            QTB = [Buf("QT%d" % i) for i in range(2)]
            wkvB, wqB = DB("wkv"), DB("wq")
            PTB = [Buf("PT%d" % i) for i in range(6)]
            masksB, btabB, rowaB = DB("masks"), DB("btab"), DB("rowa")
            ztB = [Buf("zt%d" % i) for i in range(NZT)]
            dd1B, ddB, trecB, rrtB, dsqB, ljB = Buf("dd1"), Buf("dd"), Buf("trec"), Buf("rrt"), Buf("dsq"), Buf("lj")
            trec2B = Buf("trec2")

            sp.dma(masks[:], mask_d.rearrange("p (z q) -> p z q", z=17), [], [masksB], masksB)
            sp.dma(btab[:], btab_d[:, :], [], [btabB], btabB)
            sp.dma(rowa[:], rowa_d.partition_broadcast(128), [], [rowaB], rowaB)
            STT(ljunk[:], rowa[:, 512:576], 1.0, rowa[:, 576:640], ALU.mult, ALU.mult, [rowaB], [ljB, lamB], accum=lam[:, 0:1])
            STT(ljunk[:], rowa[:, 640:704], 1.0, rowa[:, 704:768], ALU.mult, ALU.mult, [rowaB, ljB], [ljB, lamB], accum=lam[:, 1:2])
            ACT(lam[:, 0:2], lam[:, 0:2], AF.Exp, [lamB], [lamB])
            TT(dve, lam[:, 2:3], lam[:, 0:1], lam[:, 1:2], ALU.subtract, [lamB], [lamB])
            TSC(dve, lam[:, 2:3], lam[:, 2:3], LAMBDA_INIT, None, ALU.add, None, [lamB], [lamB])
            TSC(dve, lam[:, 3:4], lam[:, 2:3], -1.0, None, ALU.mult, None, [lamB], [lamB])

            pcnt = {"pt": 0, "zt": 0}
            for hp in range(2):
                sp.dma(wkv[:], s_wkv[hp].rearrange("p (k n) -> p k n", k=8), [castB["wkv%d" % hp]], [wkvB], wkvB)
                sp.dma(wq[:], s_wq[hp].rearrange("p (k n) -> p k n", k=8), [castB["wq%d" % hp]], [wqB], wqB)
                ln_part(xf[0:TS, :])
                for g in range(NG):
                    hT, hTb = tr_part(True)
                    if g + 1 < NG:
                        ln_part(xf[(g + 1) * TS:(g + 2) * TS, :])
                    else:
                        ln_part(xo[0:TS, :])
                    for hh in range(2):
                        bk, bkB = nextbank(True)
                        for kc in range(8):
                            MM(bk[:, :], wkv[:, kc, hh * 128:(hh + 1) * 128], hT[:, kc, :], kc == 0, kc == 7,
                               [wkvB, hTb[kc]], [bkB], signal=(kc == 7))
                        CP(act, KT[:, hh, g * TS:(g + 1) * TS], bk[:, :], [bkB], [KTB[g]])
                    for s in range(4):
                        bk, bkB = nextbank(True)
                        for kc in range(8):
                            MM(bk[:, 0:256], hT[:, kc, s * 128:(s + 1) * 128], wkv[:, kc, 256:512], kc == 0, kc == 7,
                               [wkvB, hTb[kc]], [bkB], signal=(kc == 7))
                        CP(dve, Vt[:, g * 4 + s, :, 0:128], bk[:, 0:256].rearrange("p (h d) -> p h d", h=2), [bkB], [VB[g]])
                    if hp == 0:
                        pool.wait(VB[g].w)
                        for _ in range(2):
                            if rest_casts:
                                d_, s_ = rest_casts.pop(0)
                                cast(d_, s_, "rest")
                tails = []
                def q_front(jq, with_ln):
                    if with_ln:
                        ln_part(xo[jq * TS:(jq + 1) * TS, :])
                    hT, hTb = tr_part(False)
                    QT, QTb = QTs[jq % 2], QTB[jq % 2]
                    for hh in range(2):
                        bk, bkB = nextbank()
                        for kc in range(8):
                            MM(bk[:, :], wq[:, kc, hh * 128:(hh + 1) * 128], hT[:, kc, :], kc == 0, kc == 7,
                               [wqB, hTb[kc]], [bkB], signal=(kc == 7))
                        CP(act, QT[:, hh, :], bk[:, :], [bkB], [QTb])

                q_front(0, False)
                for j in range(NJ):
                    QT, QTb = QTs[j % 2], QTB[j % 2]
                    nkt = 8 * j + 8
                    steps = [(hh, kt) for hh in range(2) for kt in range(nkt)]
                    sbanks = {}
                    pv_prev = [None]

                    def emit_qk(idx):
                        hh, kt = steps[idx]
                        pair = []
                        for c in range(2):
                            bk, bkB = nextbank()
                            MM(bk[:, :], KT[64 * c:64 * c + 64, hh, kt * 128:(kt + 1) * 128], QT[64 * c:64 * c + 64, hh, :],
                               True, True, [KTB[kt // 4], QTb], [bkB], signal=(c == 1))
                            pair.append((bk, bkB))
                            inflight.add(banks.index(bk))
                        sbanks[idx] = pair

                    emit_qk(0)
                    for idx, (hh, kt) in enumerate(steps):
                        if idx + 1 < len(steps):
                            emit_qk(idx + 1)
                        h = 2 * hp + hh
                        pair = sbanks.pop(idx)
                        col = bias_col(h, j, kt)
                        pts = []
                        for c in range(2):
                            bk, bkB = pair[c]
                            pi = pcnt["pt"] % 6
                            pcnt["pt"] += 1
                            pt, ptB = PTs[pi], PTB[pi]
                            src, srcB = bk, bkB
                            mi = None
                            if kt >= 8 * j:
                                mi = (kt - 8 * j) + (9 if h == 0 else 0)
                            elif h == 0:
                                mi = 8
                            if mi is not None:
                                zi = pcnt["zt"] % NZT
                                pcnt["zt"] += 1
                                TT(dve, ztmp[zi][:], bk[:, :], masks[:, mi, :], ALU.add, [bkB, masksB], [ztB[zi]])
                                src, srcB = ztmp[zi], ztB[zi]
                            ACT(pt[:, :], src[:, :], AF.Exp, [srcB, btabB], [ptB], bias=btab[:, col:col + 1], scale=0.125)
                            pts.append((pt, ptB))
                            inflight.discard(banks.index(bk))
                        for t_ in list(tails):
                            t_[0] -= 1
                            if t_[0] <= 0:
                                tails.remove(t_)
                                t_[1]()
                        def emit_pv(pts_, hh_, kt_):
                            for c in range(2):
                                pt, ptB = pts_[c]
                                Ob, ObB = banks[4 + 2 * c], bankB[4 + 2 * c]
                                Lb, LbB = banks[5 + 2 * c], bankB[5 + 2 * c]
                                MM(Ob[:, :], Vt[:, kt_, hh_, 0:128], pt[:, :], kt_ == 0, kt_ == nkt - 1, [ptB, VB[kt_ // 4]], [ObB], signal=False)
                                MM(Lb[:, :], onesb[:, :], pt[:, :], kt_ == 0, kt_ == nkt - 1, [ptB, onesB], [LbB], signal=True)

                        if pv_prev[0] is not None:
                            emit_pv(*pv_prev[0])
                            pv_prev[0] = None
                        if kt == nkt - 1:
                            emit_pv(pts, hh, kt)
                        else:
                            pv_prev[0] = (pts, hh, kt)
                        if hh == 0 and kt == nkt // 2 and j + 1 < NJ:
                            q_front(j + 1, True)
                        if kt == nkt - 1:
                            ACT(trec[:, :], banks[5][:, :], AF.Ln, [bankB[5]], [trecB])
                            ACT(trec[:, :], trec[:, :], AF.Exp, [trecB], [trecB], scale=-1.0)
                            ACT(trec2[:, :], banks[7][:, :], AF.Ln, [bankB[7]], [trec2B])
                            ACT(trec2[:, :], trec2[:, :], AF.Exp, [trec2B], [trec2B], scale=-1.0)
                            TT(dve, dd1[:, :], banks[4][:, :], trec[:, :], ALU.mult, [bankB[4], trecB], [dd1B])
                            TT(dve, trec2[:, :], banks[6][:, :], trec2[:, :], ALU.mult, [bankB[6], trec2B], [trec2B])
                            STT(dd[:, :], trec2[:, :], lam[:, 3:4], dd1[:, :], ALU.mult, ALU.add, [trec2B, lamB, dd1B], [ddB])
                            TT(dve, dsq[:, :], dd[:, :], dd[:, :], ALU.mult, [ddB], [dsqB])
                            def tail(h=h, j=j, hp=hp):
                                mb, mbB = nextbank()
                                MM(mb[:, :], onesb[:, :], dsq[:, :], True, True, [dsqB, onesB], [mbB], signal=True)
                                ACT(rrt[:, :], mb[:, :], AF.Ln, [mbB, mhB], [rrtB], bias=epsc[:, 0:1], scale=1.0 / 128.0)
                                ACT(rrt[:, :], rrt[:, :], AF.Exp, [rrtB], [rrtB], scale=-0.5)
                                STT(OT[:, h, j * TS:(j + 1) * TS], dd[:, :], colp[:, C_SG + h:C_SG + h + 1], rrt[:, :], ALU.mult, ALU.mult,
                                    [ddB, colpB, rrtB], [OTB[h][j]])
                                if hp == 0:
                                    pool.wait(OTB[h][j].w)
                                    for _ in range(2):
                                        if rest_casts:
                                            d_, s_ = rest_casts.pop(0)
                                            cast(d_, s_, "rest")
                            tails.append([3, tail])
                while tails:
                    tails.pop(0)[1]()
            while rest_casts:
                d_, s_ = rest_casts.pop(0)
                cast(d_, s_, "rest")
            if debug is not None and debug[0] == "att":
                dbgB = DB("dbg")
                dbt = sb("dbt", [128, 4, 1024], F32, ka)
                dbtB = Buf("dbt")
                for q in range(4):
                    CP(dve, dbt[:], OT[:, :, q * 1024:(q + 1) * 1024], [b for hh_ in OTB for b in hh_], [dbtB])
                    pool.dma(dbg_d[:, q * 4096:(q + 1) * 4096].rearrange("p (h t) -> p h t", h=4), dbt[:], [dbtB], [dbgB], dbgB)
                pool.wait(dbgB.w)
            barrier()
        xpool["bufs"] = xpool["bufs"][:2]

        if debug is not None and debug[0] == "att":
            pool.dma(out_d[0:128, :], xts[0][:], [xtB[0]], [castB["rest"]], castB["rest"])
            pool.wait(castB["rest"].w)
            run_all(blk, pe, act, dve, pool, sp)
            return nc, bias_cols, NBIAS

        with ExitStack() as pm:
            NSLOT = 4
            ring = [sb("ring%d" % i, [128, 4096], BF16, pm) for i in range(NSLOT)]
            ringB = [DB("ring%d" % i) for i in range(NSLOT)]
            rowp = sb("rowp", [128, 6 * D], F32, pm)
            rowpB = DB("rowp")
            Y = sb("Y", [128, 4, D], F32, pm)
            YB = [DB("Y%d" % s) for s in range(4)]
            big = sb("big", [128, 8192], F32, pm)
            hid = big[:, :].bitcast(BF16).rearrange("p (k t) -> p k t", k=32)
            uT = big[:, 0:1084].bitcast(BF16).rearrange("p (m t) -> p m t", m=4)
            uTB = [Buf("uT%d" % m) for m in range(4)]
            cbf = big[:, 1088:2112].bitcast(BF16).rearrange("p (m t) -> p m t", m=4)
            cbfB = Buf("cbf")
            csq = big[:, 2112:3136].bitcast(BF16).rearrange("p (m t) -> p m t", m=4)
            csqB = Buf("csq")
            sT = big[:, 3136:4160].bitcast(BF16).rearrange("p (m t) -> p m t", m=4)
            sTB = [Buf("sT%d" % m) for m in range(4)]
            diag = [sb("diag0", [128, 31, 128], BF16, pm)] * 2
            diagB = [Buf("diag0")] * 2
            st0 = [sb("st0_%d" % i, [128, 4, 2], F32, pm) for i in range(2)]
            st0B = [Buf("st0_%d" % i) for i in range(2)]
            mT = sb("mT", [128, 8, TS], BF16, pm)
            mTB = [Buf("mT%d" % m) for m in range(8)]
            h1T = sb("h1T", [128, 8, TS], BF16, pm)
            h1TB = [Buf("h1T_%d" % c) for c in range(8)]
            hidB = [Buf("hid%d" % k) for k in range(32)]
            smallB = uTB + [cbfB, csqB] + sTB

            def alias_fence(dsts, srcs):
                for d_ in dsts:
                    for s_ in srcs:
                        evs = list(s_.r.values()) + ([s_.w] if s_.w is not None else [])
                        for ev in evs:
                            k = id(ev.sem)
                            o = d_.r.get(k)
                            if o is None or o.val < ev.val:
                                d_.r[k] = ev
            pTt = sb("pTt", [128, 2, TS], BF16, pm)
            pTB = Buf("pTt")
            pbf = sb("pbf", [128, 4, 256], BF16, pm)
            pbfB = [Buf("pbf%d" % s) for s in range(4)]
            ftmp = [sb("ftmp%d" % i, [128, TS], F32, pm) for i in range(2)] + [big[:, 5696:6208], big[:, 6208:6720]]
            ftmpB = [Buf("ftmp%d" % i) for i in range(4)]
            rtmp = [ftmp[i][:, 0:256].bitcast(BF16) for i in range(2)]
            rtmpB = [ftmpB[i] for i in range(2)]
            hTh = sb("hTh", [128, 8, 32], BF16, pm)
            hThB = Buf("hTh")
            uh = sb("uh", [128, 64], F32, pm)
            uhB = Buf("uh")
            mstat = big[:, 4160:5696].rearrange("p (a t) -> p a t", a=3)
            mstatB = Buf("mstat")
            smallB = smallB + [mstatB, ftmpB[2], ftmpB[3]]
            fcnt = {"f": 0, "r": 0, "p": 0, "n": 4}

            def ft():
                i = fcnt["f"] % fcnt["n"]
                fcnt["f"] += 1
                return ftmp[i], ftmpB[i]

            plew = sb("plew", [128, 2, 1024], BF16, pm)
            plewB = DB("plew")
            sp.dma(plew[:], s_ple.rearrange("p (k n) -> p k n", k=2), [castB["rest"]], [plewB], plewB)
            sp.dma(rowp[:], rowp_d.partition_broadcast(128), [], [rowpB], rowpB)
            TSC(dve, rowp[:, 0:4 * D], rowp[:, 0:4 * D], ALPHA, None, ALU.mult, None, [rowpB], [rowpB])

            pieces = []
            for j in range(NJ):
                pieces.append((s_glu[0], 4096))
                pieces.append((s_glu[1], 4096))
                for m in range(8):
                    pieces.append((s_mix[m], 3072))
                pieces.append((s_wo[0], 4096))
                pieces.append((s_wo[1], 4096))
                if debug is not None and debug[0] == "pre1":
                    continue
                for n in range(8):
                    pieces.append((s_ff1[n], 4096))
                for q in range(8):
                    pieces.append((s_ff2[q], 4096))
                pieces.append((s_pg[0], 4096))
                pieces.append((s_pg[1], 4096))
            pstate = {"issued": 0, "next": 0, "rel": 0}

            def issue_to(n):
                while pstate["issued"] < min(n, len(pieces)):
                    i = pstate["issued"]
                    ap_, ncol = pieces[i]
                    sl = i % NSLOT
                    sp.dma(ring[sl][:, 0:ncol], ap_, [castB["rest"]], [ringB[sl]], ringB[sl])
                    pstate["issued"] += 1

            def next_piece():
                i = pstate["next"]
                pstate["next"] += 1
                assert i < pstate["rel"] + NSLOT
                issue_to(i + 1)
                sl = i % NSLOT
                return ring[sl], ringB[sl]

            def release(n=1):
                pstate["rel"] += n
                issue_to(pstate["rel"] + NSLOT)

            issue_to(NSLOT)

            xhbh = pbf[0:32, :, :].rearrange("p s d -> p (s d)")

            def halo_ln(j):
                k = cnt["xt"] % 2
                cnt["xt"] += 1
                xh32, xh32B = xts[k][0:32, :], xtB[k]
                sp.dma(xh32, xhalo[j * 32:(j + 1) * 32, :], [], [xh32B], xh32B)
                rstd, nb, stB = ln_stats(xh32, xh32B, 32)
                ACT(xhbh, xh32, AF.Identity, [xh32B, stB], pbfB, bias=nb, scale=rstd)

            def halo_tr(pool8):
                bk, bkB = nextbank(pool8)
                for c in range(8):
                    MM(bk[:, c * 32:(c + 1) * 32], xhbh[:, c * 128:(c + 1) * 128], identb[0:32, 0:32], True, True,
                       pbfB + [identbB], [bkB], signal=(c == 7))
                for c in range(8):
                    TSC(dve, hTh[:, c, :], bk[:, c * 32:(c + 1) * 32], colp[:, C_LN0G + c:C_LN0G + c + 1],
                        colp[:, C_LN0B + c:C_LN0B + c + 1], ALU.mult, ALU.add, [bkB, colpB], [hThB])

            DH = [(0, 16), (16, 31)]
            diagHB = [Buf("diagA"), Buf("diagB")]

            def build_diag(m, hf):
                w0, w1 = DH[hf]
                n = w1 - w0
                TT(dve, diag[0][:, w0:w1, :], identb[:, :].unsqueeze(1).broadcast_to([128, n, 128]),
                   colp[:, C_CW + m * 31 + w0:C_CW + m * 31 + w1].unsqueeze(2).broadcast_to([128, n, 128]), ALU.mult,
                   [identbB, colpB], [diagHB[hf]])

            pending = []
            nxt = make_hT(xo[0:TS, :], True, (st0[0], st0B[0]))
            halo_ln(0)
            halo_tr(True)
            for j in range(NJ):
                hT, hTb = nxt
                build_diag(0, 0)
                build_diag(0, 1)
                alias_fence(smallB, hidB)
                fcnt["n"] = 4
                wa, waB = next_piece()
                wg, wgB = next_piece()
                wa3 = wa[:, :].rearrange("p (k n) -> p k n", k=8)
                wg3 = wg[:, :].rearrange("p (k n) -> p k n", k=8)
                for m in range(4):
                    ba, baB = nextbank(True)
                    bg, bgB = nextbank(True)
                    bh, bhB = nextbank(True)
                    for kc in range(8):
                        MM(ba[:, :], wa3[:, kc, m * 128:(m + 1) * 128], hT[:, kc, :], kc == 0, kc == 7, [waB, hTb[kc]], [baB], signal=(kc == 7))
                    for kc in range(8):
                        MM(bg[:, :], wg3[:, kc, m * 128:(m + 1) * 128], hT[:, kc, :], kc == 0, kc == 7, [wgB, hTb[kc]], [bgB], signal=(kc == 7))
                    for kc in range(8):
                        MM(bh[:, 0:32], wa3[:, kc, m * 128:(m + 1) * 128], hTh[:, kc, :], kc == 0, False, [waB, hThB], [bhB], signal=False, skip=True)
                    for kc in range(8):
                        MM(bh[:, 32:64], wg3[:, kc, m * 128:(m + 1) * 128], hTh[:, kc, :], False, kc == 7, [wgB, hThB], [bhB], signal=(kc == 7), skip=True)
                    f, fB = ft()
                    ACT(f[:, :], bg[:, :], AF.Tanh, [bgB], [fB], scale=0.5)
                    STT(uT[:, m, 30:30 + TS], f[:, :], 1.0, ba[:, :], ALU.add, ALU.mult, [fB, baB], [uTB[m]])
                    ACT(uh[:, 32:64], bh[:, 32:64], AF.Tanh, [bhB], [uhB], scale=0.5)
                    STT(uh[:, 0:32], uh[:, 32:64], 1.0, bh[:, 0:32], ALU.add, ALU.mult, [uhB, bhB], [uhB])
                    TSC(dve, uT[:, m, 0:30], uh[:, 2:32], colp[:, C_HM + j:C_HM + j + 1], None, ALU.mult, None, [uhB, colpB], [uTB[m]])
                    if m % 2 == 1 and pending:
                        pending.pop(0)()
                release(2)
                cbanks = []
                for m in range(4):
                    dg = diag[0]
                    cb_, cbB_ = nextbank(True)
                    for hf in range(2):
                        w0, w1 = DH[hf]
                        for w in range(w0, w1):
                            MM(cb_[:, :], dg[:, w, :], uT[:, m, w:w + TS], w == 0, w == 30, [diagHB[hf], uTB[m]], [cbB_],
                               signal=(w == w1 - 1))
                        if m + 1 < 4:
                            build_diag(m + 1, hf)
                    cbanks.append((cb_, cbB_))
                    if m % 2 == 0 and pending:
                        pending.pop(0)()
                    if m >= 1:
                        CP(act, cbf[:, m - 1, :], cbanks[m - 1][0][:, :], [cbanks[m - 1][1]], [cbfB])
                        ACT(csq[:, m - 1, :], cbanks[m - 1][0][:, :], AF.Square, [cbanks[m - 1][1]], [csqB])
                while pending:
                    pending.pop(0)()
                CP(act, cbf[:, 3, :], cbanks[3][0][:, :], [cbanks[3][1]], [cbfB])
                ACT(csq[:, 3, :], cbanks[3][0][:, :], AF.Square, [cbanks[3][1]], [csqB])
                bm, bmB = nextbank(True)
                bq, bqB = nextbank(True)
                for m in range(4):
                    MM(bm[:, :], onesb[:], cbf[:, m, :], m == 0, m == 3, [onesB, cbfB], [bmB], signal=(m == 3))
                for m in range(4):
                    MM(bq[:, :], onesb[:], csq[:, m, :], m == 0, m == 3, [onesB, csqB], [bqB], signal=(m == 3))
                ACT(mstat[:, 0, :], bm[:, :], AF.Copy, [bmB], [mstatB], scale=1.0 / 512.0)
                TT(dve, mstat[:, 1, :], mstat[:, 0, :], mstat[:, 0, :], ALU.mult, [mstatB], [mstatB])
                STT(mstat[:, 1, :], bq[:, :], 1.0 / 512.0, mstat[:, 1, :], ALU.mult, ALU.subtract, [bqB, mstatB], [mstatB])
                ACT(mstat[:, 1, :], mstat[:, 1, :], AF.Ln, [mstatB, mhB], [mstatB], bias=epsc[:, 0:1])
                ACT(mstat[:, 1, :], mstat[:, 1, :], AF.Exp, [mstatB], [mstatB], scale=-0.5)
                TT(dve, mstat[:, 2, :], mstat[:, 0, :], mstat[:, 1, :], ALU.mult, [mstatB], [mstatB])
                for m in range(4):
                    f, fB = ft()
                    TT(dve, f[:, :], cbanks[m][0][:, :], mstat[:, 1, :], ALU.mult, [cbanks[m][1], mstatB], [fB])
                    TT(dve, f[:, :], f[:, :], mstat[:, 2, :], ALU.subtract, [fB, mstatB], [fB])
                    TSC(dve, f[:, :], f[:, :], colp[:, C_CG + m:C_CG + m + 1], colp[:, C_CB + m:C_CB + m + 1], ALU.mult, ALU.add,
                        [fB, colpB], [fB])
                    f2, f2B = ft()
                    ACT(f2[:, :], f[:, :], AF.Tanh, [fB], [f2B], scale=0.5)
                    STT(sT[:, m, :], f2[:, :], 1.0, f[:, :], ALU.add, ALU.mult, [f2B, fB], [sTB[m]])
                sv, svB = st0[j % 2], st0B[j % 2]

                def res0_piece(s, j=j, sv=sv, svB=svB):
                    k = cnt["xt"] % 2
                    cnt["xt"] += 1
                    sp.dma(xts[k][:], xo[j * TS + s * 128:j * TS + (s + 1) * 128, :], [], [xtB[k]], xtB[k])
                    ACT(Y[:, s, :], xts[k][:], AF.Identity, [xtB[k], svB], [YB[s]], bias=sv[:, s, 1:2], scale=sv[:, s, 0:1])
                    TT(dve, Y[:, s, :], Y[:, s, :], rowp[:, 0:D], ALU.mult, [YB[s], rowpB], [YB[s]])
                    TT(dve, Y[:, s, :], Y[:, s, :], rowp[:, D:2 * D], ALU.add, [YB[s], rowpB], [YB[s]])

                for m in range(8):
                    wm, wmB = next_piece()
                    byc, bycB = nextbank(True)
                    bya, byaB = nextbank(True)
                    bgc, bgcB = nextbank(True)
                    bga, bgaB = nextbank(True)
                    for kc in range(8):
                        MM(bgc[:, :], wm[:, 1024 + kc * 128:1024 + (kc + 1) * 128], hT[:, kc, :], kc == 0, kc == 7, [wmB, hTb[kc]], [bgcB], signal=(kc == 7))
                    for kc in range(8):
                        MM(bga[:, :], wm[:, 2048 + kc * 128:2048 + (kc + 1) * 128], hT[:, kc, :], kc == 0, kc == 7, [wmB, hTb[kc]], [bgaB], signal=(kc == 7))
                    for kc in range(4):
                        MM(bya[:, :], wm[:, 512 + kc * 128:512 + (kc + 1) * 128], OT[:, kc, j * TS:(j + 1) * TS], kc == 0, kc == 3,
                           [wmB, OTB[kc][j]], [byaB], signal=(kc == 3))
                    for kc in range(4):
                        MM(byc[:, :], wm[:, kc * 128:(kc + 1) * 128], sT[:, kc, :], kc == 0, kc == 3, [wmB, sTB[kc]], [bycB], signal=(kc == 3))
                    f1, f1B = ft()
                    f2, f2B = ft()
                    ACT(f1[:, :], bgc[:, :], AF.Tanh, [bgcB], [f1B], scale=0.5)
                    ACT(f2[:, :], bga[:, :], AF.Tanh, [bgaB], [f2B], scale=0.5)
                    STT(f1[:, :], f1[:, :], 1.0, byc[:, :], ALU.add, ALU.mult, [f1B, bycB], [f1B])
                    STT(f2[:, :], f2[:, :], 1.0, bya[:, :], ALU.add, ALU.mult, [f2B, byaB], [f2B])
                    STT(mT[:, m, :], f1[:, :], 0.5, f2[:, :], ALU.mult, ALU.add, [f1B, f2B], [mTB[m]])
                    release()
                    if 1 <= m <= 4:
                        while pending:
                            pending.pop(0)()
                        res0_piece(m - 1)
                wos = [next_piece(), next_piece()]
                ln1 = []
                prev_ln = None

                def ln1_finish(p_):
                    s_, (st_, B_) = p_
                    rstd, nb, stB = ln_stats_b(st_, B_)
                    ACT(xhb4[:, s_, :], Y[:, s_, :], AF.Identity, [YB[s_], stB], [xhbB[s_]], bias=nb, scale=rstd)
                    ln1.append((rstd, nb, stB))
                for s in range(4):
                    for half in range(2):
                        wo_, woB = wos[half]
                        wo3 = wo_[:, :].rearrange("p (k n) -> p k n", k=8)
                        bk, bkB = nextbank(True)
                        for kc in range(8):
                            MM(bk[:, :], mT[:, kc, s * 128:(s + 1) * 128], wo3[:, kc, :], kc == 0, kc == 7, [woB, mTB[kc]], [bkB], signal=(kc == 7))
                        STT(Y[:, s, half * 512:(half + 1) * 512], bk[:, :], 0.5, Y[:, s, half * 512:(half + 1) * 512], ALU.mult, ALU.add,
                            [bkB, YB[s]], [YB[s]])
                    cur = (s, ln_stats_a(Y[:, s, :], YB[s], 128))
                    if prev_ln is not None:
                        ln1_finish(prev_ln)
                    prev_ln = cur
                ln1_finish(prev_ln)
                release(2)
                transposes_to(h1T, h1TB, C_LN1G, C_LN1B, True)
                for s in range(4):
                    rstd, nb, stB = ln1[s]
                    ACT(Y[:, s, :], Y[:, s, :], AF.Identity, [YB[s], stB], [YB[s]], bias=nb, scale=rstd)
                    TT(dve, Y[:, s, :], Y[:, s, :], rowp[:, 2 * D:3 * D], ALU.mult, [YB[s], rowpB], [YB[s]])
                    TT(dve, Y[:, s, :], Y[:, s, :], rowp[:, 3 * D:4 * D], ALU.add, [YB[s], rowpB], [YB[s]])
                alias_fence(hidB, smallB)
                fcnt["n"] = 2
                for n in range(8):
                    w1, w1B = next_piece()
                    w13 = w1[:, :].rearrange("p (k n) -> p k n", k=8)
                    for i in range(4):
                        bk, bkB = nextbank()
                        for kc in range(8):
                            MM(bk[:, :], w13[:, kc, i * 128:(i + 1) * 128], h1T[:, kc, :], kc == 0, kc == 7, [w1B, h1TB[kc]], [bkB], signal=(kc == 7))
                        ri = fcnt["r"] % 2
                        fcnt["r"] += 1
                        ACT(rtmp[ri][:, :], bk[:, :], AF.Relu, [bkB], [rtmpB[ri]])
                        k_ = n * 4 + i
                        TT(dve, hid[:, k_, :], rtmp[ri][:, :], rtmp[ri][:, :], ALU.mult, [rtmpB[ri]], [hidB[k_]])
                    release()
                if j + 1 < NJ:
                    ln_part(xo[(j + 1) * TS:(j + 2) * TS, :], (st0[(j + 1) % 2], st0B[(j + 1) % 2]))
                    halo_ln(j + 1)
                for half in range(2):
                    for kg in range(4):
                        w2, w2B = next_piece()
                        w23 = w2[:, :].rearrange("p (k n) -> p k n", k=8)
                        for s in range(4):
                            for k in range(8):
                                MM(banks[4 + s][:, :], hid[:, kg * 8 + k, s * 128:(s + 1) * 128], w23[:, k, :],
                                   (kg == 0 and k == 0), (kg == 3 and k == 7), [w2B, hidB[kg * 8 + k]], [bankB[4 + s]], signal=(k == 7))
                        release()
                    for s in range(4):
                        TT(dve, Y[:, s, half * 512:(half + 1) * 512], banks[4 + s][:, :], Y[:, s, half * 512:(half + 1) * 512], ALU.add,
                           [bankB[4 + s], YB[s]], [YB[s]])
                if j + 1 < NJ:
                    nxt = tr_part(False)
                    halo_tr(False)
                for s in range(4):
                    pi = cnt["xt"] % 2
                    cnt["xt"] += 1
                    sp.dma(xts[pi][:, 0:256], po[j * TS + s * 128:j * TS + (s + 1) * 128, :], [], [xtB[pi]], xtB[pi])
                    CP(act, pbf[:, s, :], xts[pi][:, 0:256], [xtB[pi]], [pbfB[s]])
                for c2 in range(2):
                    bk, bkB = nextbank()
                    for s in range(4):
                        MM(bk[:, s * 128:(s + 1) * 128], pbf[:, s, c2 * 128:(c2 + 1) * 128], identb[:], True, True, [pbfB[s], identbB], [bkB], signal=(s == 3))
                    CP(dve, pTt[:, c2, :], bk[:, :], [bkB], [pTB])
                wpl3, wplB = plew, plewB
                for half in range(2):
                    wpg, wpgB = next_piece()
                    wpg3 = wpg[:, :].rearrange("p (k n) -> p k n", k=8)
                    for s in range(4):
                        bgt, bgtB = nextbank()
                        bpl, bplB = nextbank()
                        for kc in range(8):
                            MM(bgt[:, :], h1T[:, kc, s * 128:(s + 1) * 128], wpg3[:, kc, :], kc == 0, kc == 7, [wpgB, h1TB[kc]], [bgtB], signal=(kc == 7))
                        for c2 in range(2):
                            MM(bpl[:, :], pTt[:, c2, s * 128:(s + 1) * 128], wpl3[:, c2, half * 512:(half + 1) * 512], c2 == 0, c2 == 1,
                               [wplB, pTB], [bplB], signal=(c2 == 1))
                        f, fB = ft()
                        ACT(f[:, :], bgt[:, :], AF.Tanh, [bgtB], [fB], scale=0.5)
                        STT(f[:, :], f[:, :], 1.0, bpl[:, :], ALU.add, ALU.mult, [fB, bplB], [fB])
                        STT(Y[:, s, half * 512:(half + 1) * 512], f[:, :], 0.5, Y[:, s, half * 512:(half + 1) * 512], ALU.mult, ALU.add,
                            [fB, YB[s]], [YB[s]])
                release(2)
                def ln2_store(s, j=j):
                    rstd, nb, stB = ln_stats(Y[:, s, :], YB[s], 128)
                    ACT(Y[:, s, :], Y[:, s, :], AF.Identity, [YB[s], stB], [YB[s]], bias=nb, scale=rstd)
                    TT(dve, Y[:, s, :], Y[:, s, :], rowp[:, 4 * D:5 * D], ALU.mult, [YB[s], rowpB], [YB[s]])
                    TT(dve, Y[:, s, :], Y[:, s, :], rowp[:, 5 * D:6 * D], ALU.add, [YB[s], rowpB], [YB[s]])
                    pool.dma(out_d[j * TS + s * 128:j * TS + (s + 1) * 128, :], Y[:, s, :], [YB[s]], [], YB[s])
                for s in range(4):
                    pending.append(lambda s=s, f=ln2_store: f(s))
            while pending:
                pending.pop(0)()
            for s in range(4):
                pool.wait(Ev(YB[s].sem, YB[s].semcnt))
            barrier()
        run_all(blk, pe, act, dve, pool, sp)
    return nc, bias_cols, NBIAS


def run_all(blk, pe, act, dve, pool, sp):
    @blk.sync
    def _(e):
        sp.replay(e)

    @blk.tensor
    def _(e):
        pe.replay(e)

    @blk.scalar
    def _(e):
        act.replay(e)

    @blk.vector
    def _(e):
        dve.replay(e)

    @blk.gpsimd
    def _(e):
        pool.replay(e)


def _core_tables(r, bias_cols, NBIAS):
    p = np.arange(128, dtype=np.float64)
    btab = np.zeros((128, NBIAS), np.float32)
    for (h, j, kt), col in bias_cols.items():
        g = 2 * j + r
        kpos = kt * 128 + p
        if r == 0 and kt >= 8 * j + 4:
            btab[:, col] = NEGB
        else:
            btab[:, col] = SLOPES[h] * (kpos - (g * TS + 256))
    ki = np.arange(128)[:, None]
    qi = np.arange(TS)[None, :]
    masks = np.zeros((128, 17, TS), np.float32)
    shift = np.broadcast_to(-8.0 * SLOPES[0] * (qi - 256), (128, TS))
    masks[:, 8, :] = shift
    for u in range(4):
        causal = np.where(128 * u + ki <= qi, 0.0, NEGM)
        z = u if r == 0 else 4 + u
        masks[:, z, :] = causal
    for z in range(8):
        masks[:, 9 + z, :] = masks[:, z, :] + shift
    return btab, masks.reshape(128, 17 * TS).astype(ml_dtypes.bfloat16)


def _prep_inputs(inp, bias_cols, NBIAS):
    f = lambda a: np.ascontiguousarray(np.asarray(a, dtype=np.float32))
    x = f(inp["x"])
    p = f(inp["p"])[0]
    colT = lambda v: f(v).reshape(-1, 128).T
    conv_w = f(inp["conv_w"])[0]
    cw = np.concatenate([conv_w[:, m * 128:(m + 1) * 128].T for m in range(4)], axis=1)
    rowp = np.concatenate([f(inp["ln0_g"]), f(inp["ln0_b"]), f(inp["ln1_g"])[0], f(inp["ln1_b"])[0],
                           f(inp["ln2_g"])[0], f(inp["ln2_b"])[0]])
    rowa = np.concatenate([f(inp["subln_g"])[0], f(inp["lambda_q1"])[0], f(inp["lambda_k1"])[0],
                           f(inp["lambda_q2"])[0], f(inp["lambda_k2"])[0]])
    shared = {
        "w_in": f(inp["w_in"])[0], "w_co": f(inp["w_conv_out"])[0], "w_ao": f(inp["w_attn_out"])[0],
        "w_o": f(inp["w_o"])[0], "w_ff1": f(inp["w_ff1"])[0], "w_ff2": f(inp["w_ff2"])[0],
        "w_ple": f(inp["w_ple"])[0], "w_pg": f(inp["w_ple_gate"])[0],
        "rowp": f(rowp), "rowa": f(rowa), "ident": np.eye(128, dtype=np.float32),
    }
    in_maps = []
    for c in range(8):
        b, r = c // 2, c % 2
        xb = x[b]
        tiles = xb.reshape(NG, TS, D)
        own = [2 * j + r for j in range(NJ)]
        xo = np.ascontiguousarray(tiles[own].reshape(NJ * TS, D))
        halo = np.zeros((NJ, 32, D), np.float32)
        hm = np.zeros(NJ, np.float32)
        for j, g in enumerate(own):
            if g > 0:
                halo[j] = xb[g * TS - 32:g * TS]
                hm[j] = 1.0
        colp = np.concatenate([colT(inp["ln0_g"]), colT(inp["ln0_b"]), colT(f(inp["ln1_g"])[0]), colT(f(inp["ln1_b"])[0]),
                               colT(f(inp["conv_ln_g"])[0]), colT(f(inp["conv_ln_b"])[0]), cw,
                               np.tile(hm[None, :], (128, 1)), colT(f(inp["subln_g"])[0])], axis=1)
        assert colp.shape == (128, NCOL)
        btab, masks = _core_tables(r, bias_cols, NBIAS)
        m = dict(shared)
        m.update({"xf": np.ascontiguousarray(xb), "xo": xo, "xhalo": np.ascontiguousarray(halo.reshape(NJ * 32, D)),
                  "po": np.ascontiguousarray(p[b].reshape(NG, TS, 256)[own].reshape(NJ * TS, 256)),
                  "colp": f(colp), "btab": btab, "masks": masks})
        in_maps.append(m)
    return in_maps


_CACHE = {}


def kernel(**inputs):
    if "prog" not in _CACHE:
        _CACHE["prog"] = build_program(DEBUG)
    nc, bias_cols, NBIAS = _CACHE["prog"]
    in_maps = _prep_inputs(inputs, bias_cols, NBIAS)
    res = run_bass_kernel_spmd(nc, in_maps, core_ids=list(range(8)))
    out = np.zeros((4, S, D), np.float32)
    for c in range(8):
        b, r = c // 2, c % 2
        o = np.asarray(res.results[c]["out"]).reshape(NJ, TS, D)
        for j in range(NJ):
            g = 2 * j + r
            out[b, g * TS:(g + 1) * TS] = o[j]
    if DEBUG is not None:
        _CACHE["dbg"] = [np.asarray(r_.get("dbg")) if "dbg" in r_ else None for r_ in res.results]
    return out
```
